# Optimizing a Trainium2 kernel written in Bass

```python
import jax, jax.numpy as jnp
from jax import lax
import numpy as np

D_MODEL = 1024
BATCH = 8
SEQ = 4096
DEPTH = 1

D_MIX = D_MODEL
RWKV_WIDTH = D_MIX // 2
RWKV_HEAD = 64
RWKV_HEADS = RWKV_WIDTH // RWKV_HEAD
GMLP_WIDTH = D_MIX - RWKV_WIDTH
GMLP_GROUPS = 8
GMLP_GROUP_DIM = GMLP_WIDTH // GMLP_GROUPS
CHUNK = 128
DECAY_LORA = 64
AAA_LORA = 64
GATE_LORA = 128
D_FF = 2816
RMS_EPS = 1e-6
GN_EPS = 64e-5
LN_EPS = 1e-5
FFN_RES_SCALE = 0.5
D_SHIFT = 3 * RWKV_WIDTH + DECAY_LORA + AAA_LORA + GATE_LORA
D_IN = D_SHIFT + 2 * GMLP_WIDTH

kernel_name = "hymba_rwkv7_gmlp_macaron"


def rmsnorm(x, g):
    xf = x.astype(jnp.float32)
    y = xf * lax.rsqrt(jnp.mean(xf * xf, axis=-1, keepdims=True) + RMS_EPS)
    return (y * g.astype(jnp.float32)).astype(x.dtype)


def swiglu(h, w1, w3, w2):
    return (jax.nn.silu(h @ w1) * (h @ w3)) @ w2


def token_shift(p):
    return jnp.pad(p[:, :-1], ((0, 0), (1, 0), (0, 0)))


def rwkv7_scan(r, decay, k, v, a, b):
    seq_first = lambda t: jnp.transpose(t, (1, 0, 2, 3))
    B, _, H, N = r.shape

    def step(state, inp):
        r_t, w_t, k_t, v_t, a_t, b_t = inp
        sa = jnp.einsum('bhij,bhj->bhi', state, a_t)
        state = (state * w_t[:, :, None, :]
                 + sa[..., None] * b_t[:, :, None, :]
                 + v_t[..., None] * k_t[:, :, None, :])
        y_t = jnp.einsum('bhij,bhj->bhi', state, r_t)
        return state, y_t

    s0 = jnp.zeros((B, H, N, N), jnp.float32)
    _, ys = lax.scan(step, s0, tuple(seq_first(t) for t in (r, decay, k, v, a, b)))
    return jnp.transpose(ys, (1, 0, 2, 3))


def rwkv7_mixer(p, mu, w0, w_up, a0, a_up, g_up, k_k, k_a, r_k, gn_w, gn_b):
    B, S, _ = p.shape
    p = p + (token_shift(p) - p) * mu
    o1 = RWKV_WIDTH
    o2, o3 = 2 * o1, 3 * o1
    o4, o5 = o3 + DECAY_LORA, o3 + DECAY_LORA + AAA_LORA
    r, k, v = p[..., :o1], p[..., o1:o2], p[..., o2:o3]
    pw, pa, pg = p[..., o3:o4], p[..., o4:o5], p[..., o5:]
    w = -jax.nn.softplus(-(w0 + jnp.tanh(pw) @ w_up)) - 0.5
    decay = jnp.exp(-jnp.exp(w.astype(jnp.float32)))
    a = jax.nn.sigmoid(a0 + pa @ a_up)
    g = jax.nn.sigmoid(pg) @ g_up
    heads = lambda t: t.reshape(B, S, RWKV_HEADS, RWKV_HEAD).astype(jnp.float32)
    kk = heads(k * k_k)
    kk = kk / jnp.maximum(jnp.sqrt(jnp.sum(kk * kk, axis=-1, keepdims=True)), 1e-12)
    k = k * (1 + (a - 1) * k_a)
    rh, kh, vh, ah = heads(r), heads(k), heads(v), heads(a)
    y = rwkv7_scan(rh, heads(decay), kh, vh, -kk, kk * ah)
    mean = jnp.mean(y, axis=-1, keepdims=True)
    var = jnp.mean(jnp.square(y - mean), axis=-1, keepdims=True)
    y = (y - mean) * lax.rsqrt(var + GN_EPS)
    y = y * gn_w.reshape(RWKV_HEADS, RWKV_HEAD) + gn_b.reshape(RWKV_HEADS, RWKV_HEAD)
    bonus = jnp.sum(rh * kh * r_k.astype(jnp.float32), axis=-1, keepdims=True) * vh
    y = (y + bonus).reshape(B, S, RWKV_WIDTH).astype(p.dtype)
    return y * g


def gmlp_mixer(pu, pv, ln_g, ln_b, w_s, b_s):
    B, S, _ = pu.shape
    u = jax.nn.gelu(pu, approximate=False)
    vf = jax.nn.gelu(pv, approximate=False).astype(jnp.float32)
    mean = jnp.mean(vf, axis=-1, keepdims=True)
    var = jnp.mean(jnp.square(vf - mean), axis=-1, keepdims=True)
    vf = (vf - mean) * lax.rsqrt(var + LN_EPS) * ln_g + ln_b
    vc = vf.reshape(B, S // CHUNK, CHUNK, GMLP_GROUPS, GMLP_GROUP_DIM).astype(pu.dtype)
    w_causal = jnp.tril(w_s)
    mixed = jnp.einsum('gts,bcsgd->bctgd', w_causal, vc)
    mixed = mixed + jnp.transpose(b_s)[None, None, :, :, None]
    return u * mixed.reshape(B, S, GMLP_WIDTH)


def setup_inputs(seed: int = 0) -> dict:
    key = jax.random.key(seed)
    ks = jax.random.split(key, 32)
    f32 = jnp.float32
    nrm = lambda k, shape, s: jax.random.normal(k, shape, f32) * s
    L = DEPTH
    return {
        "x": jax.random.normal(ks[0], (BATCH, SEQ, D_MODEL), f32),
        "ffn1_norm": 1.0 + nrm(ks[1], (L, D_MODEL), 0.02),
        "ffn1_w1": nrm(ks[2], (L, D_MODEL, D_FF), D_MODEL ** -0.5),
        "ffn1_w3": nrm(ks[3], (L, D_MODEL, D_FF), D_MODEL ** -0.5),
        "ffn1_w2": nrm(ks[4], (L, D_FF, D_MODEL), D_FF ** -0.5),
        "mix_norm": 1.0 + nrm(ks[5], (L, D_MODEL), 0.02),
        "w_in": nrm(ks[6], (L, D_MODEL, D_IN), D_MODEL ** -0.5),
        "mu_shift": jax.random.uniform(ks[7], (L, D_SHIFT), f32),
        "w0": jax.random.uniform(ks[8], (L, RWKV_WIDTH), f32, -6.0, 0.0),
        "w_lora_up": nrm(ks[9], (L, DECAY_LORA, RWKV_WIDTH), 0.1 * DECAY_LORA ** -0.5),
        "a0": nrm(ks[10], (L, RWKV_WIDTH), 0.1),
        "a_lora_up": nrm(ks[11], (L, AAA_LORA, RWKV_WIDTH), 0.1 * AAA_LORA ** -0.5),
        "g_lora_up": nrm(ks[12], (L, GATE_LORA, RWKV_WIDTH), GATE_LORA ** -0.5),
        "k_k": 0.85 + nrm(ks[13], (L, RWKV_WIDTH), 0.02),
        "k_a": 1.0 + nrm(ks[14], (L, RWKV_WIDTH), 0.02),
        "r_k": nrm(ks[15], (L, RWKV_HEADS, RWKV_HEAD), 0.1),
        "gn_w": 1.0 + nrm(ks[16], (L, RWKV_WIDTH), 0.02),
        "gn_b": nrm(ks[17], (L, RWKV_WIDTH), 0.02),
        "sgu_ln_g": 1.0 + nrm(ks[18], (L, GMLP_WIDTH), 0.02),
        "sgu_ln_b": nrm(ks[19], (L, GMLP_WIDTH), 0.02),
        "sgu_w": nrm(ks[20], (L, GMLP_GROUPS, CHUNK, CHUNK), CHUNK ** -0.5),
        "sgu_b": 1.0 + nrm(ks[21], (L, GMLP_GROUPS, CHUNK), 0.02),
        "w_out": nrm(ks[22], (L, D_MIX, D_MODEL), D_MIX ** -0.5),
        "ffn2_norm": 1.0 + nrm(ks[23], (L, D_MODEL), 0.02),
        "ffn2_w1": nrm(ks[24], (L, D_MODEL, D_FF), D_MODEL ** -0.5),
        "ffn2_w3": nrm(ks[25], (L, D_MODEL, D_FF), D_MODEL ** -0.5),
        "ffn2_w2": nrm(ks[26], (L, D_FF, D_MODEL), D_FF ** -0.5),
        "final_norm": 1.0 + nrm(ks[27], (D_MODEL,), 0.02),
    }


def reference(x, ffn1_norm, ffn1_w1, ffn1_w3, ffn1_w2, mix_norm, w_in, mu_shift, w0, w_lora_up,
              a0, a_lora_up, g_lora_up, k_k, k_a, r_k, gn_w, gn_b, sgu_ln_g, sgu_ln_b, sgu_w,
              sgu_b, w_out, ffn2_norm, ffn2_w1, ffn2_w3, ffn2_w2, final_norm):
    for l in range(DEPTH):
        x = x + FFN_RES_SCALE * swiglu(rmsnorm(x, ffn1_norm[l]), ffn1_w1[l], ffn1_w3[l], ffn1_w2[l])
        h = rmsnorm(x, mix_norm[l])
        p = h @ w_in[l]
        y_rwkv = rwkv7_mixer(p[..., :D_SHIFT], mu_shift[l], w0[l], w_lora_up[l], a0[l],
                             a_lora_up[l], g_lora_up[l], k_k[l], k_a[l], r_k[l], gn_w[l], gn_b[l])
        y_gmlp = gmlp_mixer(p[..., D_SHIFT:D_SHIFT + GMLP_WIDTH], p[..., D_SHIFT + GMLP_WIDTH:],
                            sgu_ln_g[l], sgu_ln_b[l], sgu_w[l], sgu_b[l])
        x = x + jnp.concatenate([y_rwkv, y_gmlp], axis=-1) @ w_out[l]
        x = x + FFN_RES_SCALE * swiglu(rmsnorm(x, ffn2_norm[l]), ffn2_w1[l], ffn2_w3[l], ffn2_w2[l])
    return rmsnorm(x, final_norm)
```

```python
import numpy as np
import concourse.bass as bass
import concourse.mybir as mybir
from concourse.bass_utils import run_bass_kernel_spmd

F32 = mybir.dt.float32
BF16 = mybir.dt.bfloat16
AF = mybir.ActivationFunctionType
ALU = mybir.AluOpType
AX = mybir.AxisListType

D = 1024
SEQ = 4096
NB = 8
DFF = 2816
KC = 8
FC = 22
T = 512
TM = 256
NT = SEQ // T
C = 64
NH = 8
NWB = 2
DEFER = False
INC_STATS = False
STRICT_SAME_ENGINE = True
HOOK_K = 5
POOLC = "dve"
RMS_EPS = 1e-6
GN_EPS = 64e-5
LN_EPS = 1e-5
EM05 = float(np.exp(-0.5))

PV_N1, PV_NM, PV_N2, PV_NF = 0, 8, 16, 24
PV_MU = 32
PV_W0 = 46
PV_A0 = 50
PV_KK = 54
PV_KA = 58
PV_RK = 62
PV_GNW = 66
PV_GNB = 70
PV_LNG = 74
PV_LNB = 78
PV_N = 82
CO_ID = 0
CO_ONEB = 128
CO_MQM = 256
CO_MSL = 384
CO_IB = 448
CO_MWS = 512
CO_N = 640

WIN_SLABS = [(12, 13), (0, 4), (1, 5), (2, 6), (3, 7), (8, 9), (10, 11), (14, 15), (16, 17), (18, 19), (20, 21)]


def _flat(keys):
    out = []
    for k in keys:
        if isinstance(k, list):
            out.extend(_flat(k))
        else:
            out.append(k)
    return out


class _Op:
    __slots__ = ("eng", "fn", "deps", "dma", "needs_inc", "ticket", "clock", "sem", "semval", "idx", "tag")


class Sched:
    ENGS = ("pe", "act", "dve", "pool", "sp")
    NDMA = {"pool": 8, "sp": 8, "act": 2}

    def __init__(self):
        self.ops = []
        self.last_w = {}
        self.readers = {}
        self.dma_hist = {e: [] for e in self.NDMA}
        self.tag = ""
        self.hook = None
        self.last_pe = None
        self.last_pe_cls = None
        self._in_hook = False
        self._hcnt = 0

    def add(self, eng, fn, r=(), w=(), dma=False, pcls=None):
        op = _Op()
        op.eng, op.fn, op.dma = eng, fn, dma
        op.needs_inc = False
        op.tag = self.tag
        op.idx = len(self.ops)
        r = _flat(r)
        w = _flat(w)
        deps = {}
        for k in r:
            d = self.last_w.get(k)
            if d is not None:
                deps[d.idx] = (d, True)
        for k in w:
            d = self.last_w.get(k)
            if d is not None and d.idx not in deps:
                deps[d.idx] = (d, False)
            lastr = {}
            for d in self.readers.get(k, ()):
                if d.dma:
                    if d.idx not in deps:
                        deps[d.idx] = (d, False)
                else:
                    lastr[d.eng] = d
            for d in lastr.values():
                if d.idx not in deps:
                    deps[d.idx] = (d, False)
        dl = []
        for d, raw in deps.values():
            if d.dma:
                dl.append(d)
                continue
            if d.eng == eng and not dma:
                if eng == "pe":
                    continue
                if not raw and not STRICT_SAME_ENGINE:
                    continue
            dl.append(d)
        if eng == "pe" and fn is not None:
            if self.last_pe is not None and pcls != self.last_pe_cls:
                dl.append(self.last_pe)
            self.last_pe, self.last_pe_cls = op, pcls
        if dma:
            h = self.dma_hist[eng]
            n = self.NDMA[eng]
            op.sem = len(h) % n
            op.semval = 16 * (len(h) // n + 1)
            if len(h) >= n:
                dl.append(h[len(h) - n])
            h.append(op)
        for d in dl:
            d.needs_inc = True
        op.deps = dl
        for k in r:
            self.readers.setdefault(k, []).append(op)
        for k in w:
            self.last_w[k] = op
            self.readers[k] = []
        self.ops.append(op)
        if self.hook is not None and not self._in_hook and eng != "pe":
            self._hcnt += 1
            if self._hcnt % HOOK_K == 0:
                self._in_hook = True
                self.hook()
                self._in_hook = False
        return op

    def emit(self, nc, sems, dsems, engines):
        cnt = {e: 0 for e in self.ENGS}
        for op in self.ops:
            if op.dma:
                op.needs_inc = True
            elif op.needs_inc:
                cnt[op.eng] += 1
                op.ticket = cnt[op.eng]
        known = {e: {} for e in self.ENGS}
        streams = {e: [] for e in self.ENGS}
        for op in self.ops:
            kn = known[op.eng]
            waits = []
            for d in op.deps:
                key = ("d", d.eng, d.sem) if d.dma else d.eng
                val = d.semval if d.dma else d.ticket
                if kn.get(key, 0) >= val:
                    continue
                waits.append((key, val))
                for k2, v2 in d.clock.items():
                    if kn.get(k2, 0) < v2:
                        kn[k2] = v2
            wm = {}
            for k, v in waits:
                if wm.get(k, 0) < v:
                    wm[k] = v
            if op.needs_inc:
                clk = dict(kn)
                if op.dma:
                    clk[("d", op.eng, op.sem)] = op.semval
                else:
                    clk[op.eng] = op.ticket
                op.clock = clk
            streams[op.eng].append((op, list(wm.items())))
        self.counts = cnt
        return streams


def _replay(e, eng_name, stream, sems, dsems):
    for op, waits in stream:
        for key, val in waits:
            if isinstance(key, tuple):
                e.wait_ge(dsems[key[1]][key[2]], val)
            else:
                e.wait_ge(sems[key], val)
        if op.fn is None:
            continue
        ins = op.fn(e)
        if op.dma:
            ins.then_inc(dsems[op.eng][op.sem], 16)
        elif op.needs_inc:
            ins.then_inc(sems[op.eng], 1)


def build_nc(dbg=None, ntiles=NT, upto=9, ms=9):
    nc = bass.Bass("TRN2", target_bir_lowering=False)
    dram = {}

    def din(name, shape):
        dram[name] = nc.dram_tensor(name, list(shape), F32, kind="ExternalInput").ap()
        return dram[name]

    xT = din("xT", [128, KC, SEQ])
    pv_d = din("pv", [128, PV_N])
    co_d = din("co", [128, CO_N])
    w13_d = [din(f"w13_{l}", [11, 128, 2 * KC * 256]) for l in range(2)]
    w2_d = [din(f"w2_{l}", [8, 128, FC * 128]) for l in range(2)]
    win_d = din("win", [11, 128, KC * 256])
    wout_d = din("wout", [128, KC * 1024])
    wup_d = din("wup", [64, 512])
    aup_d = din("aup", [64, 512])
    gup_d = din("gup", [128, 512])
    wst_d = din("wst", [128, 8 * 128])
    sbb_d = din("sbb", [128, 4 * 128])
    outT = nc.dram_tensor("outT", [128, KC, SEQ], F32, kind="ExternalOutput").ap()
    w13_s = [nc.dram_tensor(f"w13s_{l}", [11, 128, 2 * KC * 256], BF16, kind="Internal").ap() for l in range(2)]
    w2_s = [nc.dram_tensor(f"w2s_{l}", [8, 128, FC * 128], BF16, kind="Internal").ap() for l in range(2)]
    win_s = nc.dram_tensor("wins", [11, 128, KC * 256], BF16, kind="Internal").ap()
    dbg_d = {}
    if dbg:
        for name, shape in dbg.items():
            dbg_d[name] = nc.dram_tensor("dbg_" + name, list(shape), F32, kind="ExternalOutput").ap()

    S = Sched()
    import contextlib
    es = contextlib.ExitStack()
    with es:
        def sb(name, shape, dt):
            return es.enter_context(nc.sbuf_tensor("s_" + name, list(shape), dt))

        xt = sb("xt", [128, KC, T], F32)
        hn = sb("hn", [128, KC, T], BF16)
        scr = sb("scr", [128, FC * T], BF16)
        w13b = [sb(f"w13b{i}", [128, 2, KC, 256], BF16) for i in range(NWB)]
        w2b = [sb(f"w2b{i}", [128, FC, 128], BF16) for i in range(NWB)]
        woutb = sb("woutb", [128, KC, 1024], BF16)
        wupb = sb("wupb", [64, 512], BF16)
        aupb = sb("aupb", [128, 512], BF16)
        gupb = sb("gupb", [128, 512], BF16)
        wstb = sb("wstb", [128, 8, 128], BF16)
        sbb = sb("sbb", [128, 4, 128], F32)
        pv = sb("pv", [128, PV_N], F32)
        pv2 = sb("pv2", [128, 32], F32)
        co = sb("co", [128, CO_N], F32)
        idb = sb("idb", [128, 128], BF16)
        onesb = sb("onesb", [128, 128], BF16)
        oneblk = sb("oneblk", [128, 128], BF16)
        mqm = sb("mqm", [128, 128], BF16)
        msl = sb("msl", [128, 64], BF16)
        iblk = sb("iblk", [128, 64], BF16)
        onesf = sb("onesf", [128, 64], F32)
        epsb = sb("epsb", [128, 4], F32)
        sq = sb("sq", [128, 2, T], BF16)
        carry = sb("carry", [128, 2, 14], F32)
        Hst = sb("Hst", [128, 4, 64], F32)
        Hb = sb("Hb", [128, 4, 64], BF16)
        HK = sb("HK", [128, 4, 64], F32)
        Ht = sb("Ht", [128, 4, 64], F32)
        praw = [sb(f"praw{i}", [128, TM + 1], F32) for i in range(2)]
        dtmp = [sb(f"dtmp{i}", [128, TM], F32) for i in range(2)]
        tanhw = sb("tanhw", [64, TM], BF16)
        palow = sb("palow", [128, TM], BF16)
        sigg = sb("sigg", [128, TM], BF16)
        Gg2 = [sb(f"Gg{i}", [128, 4, TM], BF16) for i in range(2)]
        lrp = [sb(f"lrp{i}", [128, TM], F32) for i in range(2)]
        ldt = sb("ldt", [128, TM], F32)
        clt = sb("clt", [128, TM], F32)
        cle = sb("cle", [128, TM], F32)
        epos = sb("epos", [128, TM], F32)
        eneg = sb("eneg", [128, TM], F32)
        eexc = sb("eexc", [128, TM], F32)
        alp = sb("alp", [128, TM], F32)
        kkt = sb("kkt", [128, TM], F32)
        kksq = sb("kksq", [128, TM], BF16)
        kkn = sb("kkn", [128, TM], F32)
        tmpa = sb("tmpa", [128, TM], F32)
        tmpb = sb("tmpb", [128, TM], F32)
        gam2 = [sb(f"gam{i}", [128, 4, TM // C], F32) for i in range(2)]
        AR2 = [sb(f"AR{i}", [128, 4, TM // C, 2, C], BF16) for i in range(2)]
        Bt = sb("Bt", [128, 4, TM], BF16)
        Kt = sb("Kt", [128, 4, TM], BF16)
        vTb2 = [sb(f"vTb{i}", [128, 4, TM], BF16) for i in range(2)]
        rkr2 = [sb(f"rkr{i}", [128, 4, TM], BF16) for i in range(2)]
        ug = sb("ug", [128, 4, TM], BF16)
        vf = sb("vf", [128, 4, TM], F32)
        vfb = sb("vfb", [128, 4, TM], BF16)
        vfq = sb("vfq", [128, 4, TM], BF16)
        vnb = sb("vnb", [128, 4, TM], BF16)
        lnm = sb("lnm", [128, TM], F32)
        lnr = sb("lnr", [128, TM], F32)
        NJJ = TM // 128
        NJ = TM // C
        Btok = sb("Btok", [128, NJJ, 512], BF16)
        Ktok = sb("Ktok", [128, NJJ, 512], BF16)
        Vtok = sb("Vtok", [128, NJJ, 512], BF16)
        VNtok = sb("VNtok", [128, NJJ, 512], BF16)
        QM = sb("QM", [128, NJJ, 8, 128], BF16)
        LM = sb("LM", [128, NJJ, 8, 128], BF16)
        Gtok = sb("Gtok", [128, NJJ, 512], F32)
        KVs = sb("KVs", [128, 4, NJ, 64], F32)
        Usb = sb("Usb", [128, NJJ, 512], BF16)
        ZGb = sb("ZGb", [128, NJJ, 512], BF16)
        Ysq = sb("Ysq", [128, 512], F32)
        gst = sb("gst", [128, NJJ, 4, 8], F32)
        ynT = sb("ynT", [128, TM], F32)
        mix_t = sb("mix_t", [128, TM], F32)
        cat = sb("cat", [128, KC, T], BF16)

        ps = [es.enter_context(nc.psum_tensor(f"ps{i}", [128, 512], F32)) for i in range(8)]
        sems = {e: es.enter_context(nc.semaphore(f"s_{e}")) for e in Sched.ENGS}
        dsems = {e: [es.enter_context(nc.semaphore(f"d_{e}{i}")) for i in range(n)] for e, n in Sched.NDMA.items()}

        bk = [0, 0, 0]
        bk_dom = [None]

        bk_res = set()

        def bank():
            d = bk_dom[0]
            if d is None:
                i = bk[0]
                while i in bk_res:
                    i = (i + 1) % 8
                bk[0] = (i + 1) % 8
                return i
            i = bk[1 + d]
            bk[1 + d] = (i + 1) % 4
            return 4 * d + i

        act_v = scr[:].rearrange("p (f t) -> p f t", t=T)
        ot_v = scr[:, 0:KC * T * 2].bitcast(F32).rearrange("p (k t) -> p k t", t=T)

        def scrk(base, n):
            return [("scr", base + i) for i in range(n)]

        def scr_bf(off_bytes, nbytes):
            return scr[:, off_bytes // 2:(off_bytes + nbytes) // 2]

        Pk = [scr_bf(2048 * i, 2048).rearrange("p (j t) -> p j t", t=512) for i in range(2)]
        Qk = [scr_bf(4096 + 2048 * i, 2048).rearrange("p (j t) -> p j t", t=512) for i in range(2)]
        Xk = [scr_bf(8192 + 2048 * i, 2048).rearrange("p (j t) -> p j t", t=512) for i in range(2)]
        Ya = scr_bf(12288, 4096).bitcast(F32).rearrange("p (j t) -> p j t", t=512)
        Yt = scr_bf(16384, 4096).bitcast(F32).rearrange("p (j t) -> p j t", t=512)
        wstf = scr_bf(0, 4096).bitcast(F32).rearrange("p (g t) -> p g t", t=128)
        PkK = lambda i, jj: scrk(0 + 4 * i + 2 * jj, 2)
        QkK = lambda i, jj: scrk(8 + 4 * i + 2 * jj, 2)
        XkK = lambda i, jj: scrk(16 + 4 * i + 2 * jj, 2)
        YaK = lambda jj: scrk(24 + 4 * jj, 4)
        YtK = lambda jj: scrk(32 + 4 * jj, 4)
        wstfK = scrk(0, 8)
        rstd = Ysq
        silu_t = [Gtok[:, i, :] for i in range(2)]

        def act_keys(fc):
            return [("scr", 2 * fc), ("scr", 2 * fc + 1)]

        def ot_keys(kc):
            return [("scr", 4 * kc + i) for i in range(4)]

        def pvc(col, n=1, rows=slice(0, 128)):
            return pv[rows, col:col + n]

        def dma_in(eng, out_ap, in_ap, w, r=(), cast=False):
            if cast:
                S.add(eng, lambda e: e.dma_start(out=out_ap, in_=in_ap, max_dma_last_dim=8192), r=r, w=w, dma=True)
            else:
                S.add(eng, lambda e: e.dma_start(out=out_ap, in_=in_ap), r=r, w=w, dma=True)

        wseen = set()

        def load_w(sb2d, d32, dsc, key, skey, split):
            if skey not in wseen:
                wseen.add(skey)
                dma_in("pool", sb2d.rearrange("p (a b) -> p a b", b=split), d32.rearrange("p (a b) -> p a b", b=split),
                       w=[key], cast=True)
                S.add("sp", lambda e: e.dma_start(out=dsc, in_=sb2d), r=[key], w=[skey], dma=True)
            else:
                S.add("sp", lambda e: e.dma_start(out=sb2d, in_=dsc), r=[skey], w=[key], dma=True)

        def _psz(ap):
            v = ap.partition_size
            v = v() if callable(v) else v
            return 128 if v > 64 else (64 if v > 32 else 32)

        def MM(out_ap, lhsT, rhs, r, w, start=True, stop=True):
            S.add("pe", lambda e: e.matmul(out_ap, lhsT=lhsT, rhs=rhs, start=start, stop=stop), r=r, w=w,
                  pcls=(_psz(lhsT), _psz(out_ap), "mm"))

        def TR(out_ap, in_ap, ident, r, w):
            S.add("pe", lambda e: e.transpose(out_ap, in_ap, ident), r=r, w=w, pcls=(_psz(in_ap), _psz(out_ap), "mm"))

        def ACT(out_ap, in_ap, func, r, w, bias=None, scale=None):
            kw = {}
            if bias is not None:
                kw["bias"] = bias
            if scale is not None:
                kw["scale"] = scale
            S.add("act", lambda e: e.activation(out=out_ap, in_=in_ap, func=func, **kw), r=r, w=w)

        def TT(out_ap, a, b, op, r, w, eng="dve"):
            S.add(eng, lambda e: e.tensor_tensor(out=out_ap, in0=a, in1=b, op=op), r=r, w=w)

        def TS(out_ap, a, s1, s2, op0, op1, r, w, eng="dve"):
            if op1 is None:
                S.add(eng, lambda e: e.tensor_scalar(out=out_ap, in0=a, scalar1=s1, scalar2=None, op0=op0), r=r, w=w)
            else:
                S.add(eng, lambda e: e.tensor_scalar(out=out_ap, in0=a, scalar1=s1, scalar2=s2, op0=op0, op1=op1), r=r, w=w)

        def STT(out_ap, a, s, b, op0, op1, r, w):
            S.add("dve", lambda e: e.scalar_tensor_tensor(out=out_ap, in0=a, scalar=s, in1=b, op0=op0, op1=op1), r=r, w=w)

        def CP(out_ap, in_ap, r, w, eng="dve"):
            if eng == "act":
                S.add("act", lambda e: e.copy(out=out_ap, in_=in_ap), r=r, w=w)
            else:
                S.add(eng, lambda e: e.tensor_copy(out=out_ap, in_=in_ap), r=r, w=w)

        def RECIP(out_ap, in_ap, r, w):
            S.add("dve", lambda e: e.reciprocal(out=out_ap, in_=in_ap), r=r, w=w)

        def MEMSET(ap, val, w, eng="dve"):
            S.add(eng, lambda e: e.memset(ap, val), r=(), w=w)

        def dump(name, src_ap, r):
            if dbg and name in dbg_d:
                dst = dbg_d[name]
                if src_ap.dtype != F32:
                    S.add("pool", lambda e: e.dma_start(out=dst, in_=src_ap, max_dma_last_dim=2048), r=r, w=[("dbg", name)], dma=True)
                else:
                    S.add("sp", lambda e: e.dma_start(out=dst, in_=src_ap), r=r, w=[("dbg", name)], dma=True)

        dma_in("sp", pv[:], pv_d[:, :], w=["pv"])
        dma_in("sp", co[:], co_d[:, :], w=["co"])
        dma_in("sp", sbb[:].rearrange("p c t -> p (c t)"), sbb_d[:, :], w=["sbb"])
        dma_in("sp", wstf, wst_d.rearrange("p (g t) -> p g t", t=128), w=wstfK)
        dma_in("pool", woutb[:].rearrange("p k m -> p (k m)").rearrange("p (a b) -> p a b", b=2048),
               wout_d.rearrange("p (a b) -> p a b", b=2048), w=["woutb"], cast=True)
        dma_in("pool", wupb[:], wup_d[:, :], w=["wupb"], cast=True)
        dma_in("pool", aupb[64:128, :], aup_d[:, :], w=["aupb"], cast=True)
        dma_in("pool", gupb[:], gup_d[:, :], w=["gupb"], cast=True)
        CP(idb[:], co[:, CO_ID:CO_ID + 128], r=["co"], w=["idb"])
        CP(oneblk[:], co[:, CO_ONEB:CO_ONEB + 128], r=["co"], w=["oneblk"])
        CP(mqm[:], co[:, CO_MQM:CO_MQM + 128], r=["co"], w=["mqm"])
        CP(msl[:], co[:, CO_MSL:CO_MSL + 64], r=["co"], w=["msl"])
        CP(iblk[:], co[:, CO_IB:CO_IB + 64], r=["co"], w=["iblk"])
        MEMSET(onesb[:], 1.0, w=["onesb"])
        MEMSET(onesf[:], 1.0, w=["onesf"])
        MEMSET(epsb[:, 0:1], RMS_EPS, w=["epsb"])
        MEMSET(epsb[:, 1:2], LN_EPS, w=["epsb"])
        MEMSET(epsb[:, 2:3], GN_EPS, w=["epsb"])
        MEMSET(epsb[:, 3:4], 0.0, w=["epsb"])
        MEMSET(carry[:].rearrange("p a c -> p (a c)"), 0.0, w=[("carry", a_, c_) for a_ in range(2) for c_ in range(14)])
        MEMSET(Hst[:], 0.0, w=["Hst"])
        MEMSET(Hb[:], 0.0, w=["Hb"])
        for g in range(8):
            TT(wstb[:, g, :], wstf[:, g, :], co[:, CO_MWS:CO_MWS + 128], ALU.mult, r=[wstfK, "co"], w=[("wstb", g)])
        TS(pv2[:, 0:4], pv[:, PV_KA:PV_KA + 4], -1.0, 1.0, ALU.mult, ALU.add, r=["pv"], w=["pv2"])
        TS(pv2[:, 4:8], pv[:, PV_W0:PV_W0 + 4], -1.0, None, ALU.mult, None, r=["pv"], w=["pv2"])
        TS(pv2[:, 8:12], pv[:, PV_A0:PV_A0 + 4], -1.0, None, ALU.mult, None, r=["pv"], w=["pv2"])
        TS(pv2[:, 16:30], pv[:, PV_MU:PV_MU + 14], -1.0, 1.0, ALU.mult, ALU.add, r=["pv"], w=["pv2"])

        def SIGM(out_ap, in_ap, tmp_ap, r, w, tkey, negb=None, xs=1.0):
            if negb is None:
                ACT(tmp_ap, in_ap, AF.Exp, r=r, w=[tkey], scale=-xs)
            else:
                ACT(tmp_ap, in_ap, AF.Exp, r=r + ["pv2"], w=[tkey], scale=-xs, bias=negb)
            ACT(tmp_ap, tmp_ap, AF.Ln, r=[tkey], w=[tkey], bias=1.0)
            ACT(out_ap, tmp_ap, AF.Exp, r=[tkey], w=w, scale=-1.0)

        def RSQRT(out_ap, in_ap, r, w, bias=None, scale=None):
            ACT(out_ap, in_ap, AF.Ln, r=r, w=w, bias=bias, scale=scale)
            ACT(out_ap, out_ap, AF.Exp, r=w, w=w, scale=-0.5)

        def stats_begin():
            if not INC_STATS:
                return None
            b = bank()
            bk_res.add(b)
            return b

        def stats_add(b, kc):
            if b is None:
                return
            h = kc % 2
            ACT(sq[:, h, :], xt[:, kc, :], AF.Square, r=[("xt", kc)], w=[("sq", h)])
            MM(ps[b][:], onesb[:], sq[:, h, :], r=[("sq", h), "onesb"], w=[("ps", b)], start=(kc == 0), stop=(kc == KC - 1))
            if kc == KC - 1:
                bk_res.discard(b)

        def rmsnorm_to(gcol, out_fn, tag, b=None):
            if b is None:
                b = bank()
                for kc in range(KC):
                    h = kc % 2
                    ACT(sq[:, h, :], xt[:, kc, :], AF.Square, r=[("xt", kc)], w=[("sq", h)])
                    MM(ps[b][:], onesb[:], sq[:, h, :], r=[("sq", h), "onesb"], w=[("ps", b)], start=(kc == 0), stop=(kc == KC - 1))
            RSQRT(rstd[:], ps[b][:], r=[("ps", b), "epsb"], w=["Ysq"], bias=epsb[:, 0:1], scale=1.0 / D)
            for kc in range(KC):
                o, wk = out_fn(kc)
                STT(o, xt[:, kc, :], pvc(gcol + kc), rstd[:], ALU.mult, ALU.mult, r=[("xt", kc), "Ysq", "pv"], w=wk)

        def hn_out(kc):
            return hn[:, kc, :], [("hn", kc)]

        wslot = {"w13": 0, "w2": 0}

        def ffn(l):
            for s in range(11):
                bi = wslot["w13"] % NWB
                wslot["w13"] += 1
                wb = w13b[bi]
                load_w(wb[:].rearrange("p a k c -> p (a k c)"), w13_d[l][s], w13_s[l][s], ("w13b", bi), ("w13s", l, s), 2048)
                for f2 in range(2):
                    fc = 2 * s + f2
                    bg, bu = bank(), bank()
                    for which, bnk in ((0, bg), (1, bu)):
                        for kc in range(KC):
                            MM(ps[bnk][:], wb[:, which, kc, f2 * 128:(f2 + 1) * 128], hn[:, kc, :],
                               r=[("w13b", bi), ("hn", kc)], w=[("ps", bnk)], start=(kc == 0), stop=(kc == KC - 1))
                    st = silu_t[fc % 2]
                    ACT(st, ps[bg][:], AF.Silu, r=[("ps", bg)], w=[("Gtok", fc % 2)])
                    TT(act_v[:, fc, :], st, ps[bu][:], ALU.mult, r=[("Gtok", fc % 2), ("ps", bu)], w=act_keys(fc))
            sb_ = stats_begin()
            for m in range(KC):
                bi = wslot["w2"] % NWB
                wslot["w2"] += 1
                wb = w2b[bi]
                load_w(wb[:].rearrange("p f c -> p (f c)"), w2_d[l][m], w2_s[l][m], ("w2b", bi), ("w2s", l, m), 1408)
                b = bank()
                for fc in range(FC):
                    MM(ps[b][:], wb[:, fc, :], act_v[:, fc, :], r=[("w2b", bi)] + act_keys(fc), w=[("ps", b)],
                       start=(fc == 0), stop=(fc == FC - 1))
                STT(xt[:, m, :], ps[b][:], 0.5, xt[:, m, :], ALU.mult, ALU.add, r=[("ps", b), ("xt", m)], w=[("xt", m)])
                stats_add(sb_, m)
            return sb_

        mix_ctr = [0]

        def mixer(sub, deferred_in, defer):
            t0 = sub * TM
            pq = sub % 2
            Gg, gam, AR, vTb, rkr = Gg2[pq], gam2[pq], AR2[pq], vTb2[pq], rkr2[pq]
            kGg, kgam, kAR, kvTb, krkr = f"Gg{pq}", f"gam{pq}", f"AR{pq}", f"vTb{pq}", f"rkr{pq}"
            deferred_in = list(deferred_in)
            cpar = mix_ctr[0] % 2
            mix_ctr[0] += 1

            def run_deferred(n):
                old = S.tag
                bk_dom[0] = 1
                while n > 0 and deferred_in:
                    try:
                        next(deferred_in[0])
                        n -= 1
                    except StopIteration:
                        deferred_in.pop(0)
                bk_dom[0] = 0 if deferred_in else None
                S.tag = old

            if deferred_in:
                bk_dom[0] = 0
                S.hook = lambda: run_deferred(1)
            S.tag = S.tag.split("/")[0] + "/proj"
            hs = lambda kc: hn[:, kc, t0:t0 + TM]
            hk = [("hn", kc) for kc in range(KC)]

            def lerp(psb, ch_id, out_ap, out_keys, pi):
                pr = praw[pi]
                dt_ = dtmp[pi]
                CP(pr[:, 1:TM + 1], ps[psb][:, 0:TM], r=[("ps", psb)], w=[("prawb", pi)], eng="act")
                CP(pr[:, 0:1], carry[:, cpar, ch_id:ch_id + 1], r=[("carry", cpar, ch_id)], w=[("praw0", pi)])
                CP(carry[:, 1 - cpar, ch_id:ch_id + 1], pr[:, TM:TM + 1], r=[("prawb", pi)], w=[("carry", 1 - cpar, ch_id)])
                S.add("act", lambda e: e.activation(out=dt_[:], in_=ps[psb][:, 0:TM], func=AF.Copy, scale=pv2[:, 16 + ch_id:17 + ch_id]),
                      r=[("ps", psb), "pv2"], w=[("dtmp", pi)])
                STT(out_ap, pr[:, 0:TM], pvc(PV_MU + ch_id), dt_[:], ALU.mult, ALU.add,
                    r=[("prawb", pi), ("praw0", pi), ("dtmp", pi), "pv"], w=out_keys)

            lerp_ctr = [0]
            for si, (ca, cb) in enumerate(WIN_SLABS):
                bi = wslot["w13"] % NWB
                wslot["w13"] += 1
                wb = w13b[bi]
                wv = wb[:].rearrange("p a k c -> p (a k c)")[:, 0:KC * 256]
                load_w(wv, win_d[si], win_s[si], ("w13b", bi), ("wins", si), 2048)
                wv3 = wv.rearrange("p (k c) -> p k c", c=256)
                pb2 = []
                for f2 in range(2):
                    b = bank()
                    pb2.append(b)
                    for kc in range(KC):
                        MM(ps[b][:, 0:TM], wv3[:, kc, f2 * 128:(f2 + 1) * 128], hs(kc), r=[("w13b", bi), ("hn", kc)],
                           w=[("ps", b)], start=(kc == 0), stop=(kc == KC - 1))
                if si == 0:
                    pi = lerp_ctr[0] % 2; lerp_ctr[0] += 1
                    lerp(pb2[0], 12, tmpa[:], ["tmpa"], pi)
                    SIGM(kkt[0:64, :], tmpa[0:64, :], kkt[0:64, :], r=["tmpa"], w=["kkt"], tkey="kkt", xs=2.0)
                    TS(tanhw[:], kkt[0:64, :], 2.0, -1.0, ALU.mult, ALU.add, r=["kkt"], w=["tanhw"])
                    CP(palow[64:128, :], tmpa[64:128, :], r=["tmpa"], w=["palow"], eng="act")
                    pi = lerp_ctr[0] % 2; lerp_ctr[0] += 1
                    lerp(pb2[1], 13, tmpb[:], ["tmpb"], pi)
                    SIGM(sigg[:], tmpb[:], kkn[:], r=["tmpb"], w=["sigg"], tkey="kkn")
                    for c in range(4):
                        b = bank()
                        MM(ps[b][:, 0:TM], gupb[:, c * 128:(c + 1) * 128], sigg[:], r=["gupb", "sigg"], w=[("ps", b)])
                        CP(Gg[:, c, :], ps[b][:, 0:TM], r=[("ps", b)], w=[(kGg, c)], eng="act")
                elif 1 <= si <= 4:
                    c = si - 1
                    bz = bank()
                    MM(ps[bz][:, 0:TM], wupb[:, c * 128:(c + 1) * 128], tanhw[:], r=["wupb", "tanhw"], w=[("ps", bz)])
                    ba = bank()
                    MM(ps[ba][:, 0:TM], aupb[64:128, c * 128:(c + 1) * 128], palow[64:128, :], r=["aupb", "palow"], w=[("ps", ba)])
                    SIGM(ldt[:], ps[bz][:, 0:TM], ldt[:], r=[("ps", bz)], w=["ldt"], tkey="ldt", negb=pv2[:, 4 + c:5 + c])
                    SIGM(alp[:], ps[ba][:, 0:TM], alp[:], r=[("ps", ba)], w=["alp"], tkey="alp", negb=pv2[:, 8 + c:9 + c])
                    for j in range(NJ):
                        S.add("dve", (lambda j=j: (lambda e: e.tensor_tensor_scan(
                            out=clt[:, j * C:(j + 1) * C], data0=onesf[:, 0:C], data1=ldt[:, j * C:(j + 1) * C],
                            initial=0.0, op0=ALU.mult, op1=ALU.add)))(), r=["ldt", "onesf"], w=["clt"])
                    TT(cle[:], clt[:], ldt[:], ALU.subtract, r=["clt", "ldt"], w=["cle"])
                    ACT(epos[:], clt[:], AF.Exp, r=["clt"], w=["epos"], scale=-EM05)
                    ACT(eneg[:], clt[:], AF.Exp, r=["clt"], w=["eneg"], scale=EM05)
                    ACT(eexc[:], cle[:], AF.Exp, r=["cle"], w=["eexc"], scale=-EM05)
                    CP(gam[:, c, :], epos[:].rearrange("p (j t) -> p j t", t=C)[:, :, C - 1], r=["epos"], w=[(kgam, c)])
                    pi = lerp_ctr[0] % 2; lerp_ctr[0] += 1
                    lerp(pb2[0], ca, lrp[0][:], [("lrp", 0)], pi)
                    pi = lerp_ctr[0] % 2; lerp_ctr[0] += 1
                    lerp(pb2[1], cb, lrp[1][:], [("lrp", 1)], pi)
                    rr, kk_ = lrp[0], lrp[1]
                    TS(kkt[:], kk_[:], pvc(PV_KK + c), None, ALU.mult, None, r=[("lrp", 1), "pv"], w=["kkt"])
                    ACT(kksq[:], kkt[:], AF.Square, r=["kkt"], w=["kksq"])
                    bn = bank()
                    MM(ps[bn][:, 0:TM], oneblk[:], kksq[:], r=["oneblk", "kksq"], w=[("ps", bn)])
                    TS(kkn[:], ps[bn][:, 0:TM], 1e-24, None, ALU.max, None, r=[("ps", bn)], w=["kkn"])
                    RSQRT(kkn[:], kkn[:], r=["kkn"], w=["kkn"])
                    TT(kkn[:], kkn[:], kkt[:], ALU.mult, r=["kkn", "kkt"], w=["kkn"])
                    TS(tmpa[:], alp[:], pvc(PV_KA + c), pv2[:, c:c + 1], ALU.mult, ALU.add, r=["alp", "pv", "pv2"], w=["tmpa"], eng=POOLC)
                    TT(tmpa[:], tmpa[:], kk_[:], ALU.mult, r=["tmpa", ("lrp", 1)], w=["tmpa"], eng=POOLC)
                    STT(rkr[:, c, :], rr[:], pvc(PV_RK + c), tmpa[:], ALU.mult, ALU.mult, r=[("lrp", 0), "tmpa", "pv"], w=[(krkr, c)])
                    TT(Kt[:, c, :], tmpa[:], eneg[:], ALU.mult, r=["tmpa", "eneg"], w=[("Kt", c)], eng=POOLC)
                    TT(tmpb[:], kkn[:], alp[:], ALU.mult, r=["kkn", "alp"], w=["tmpb"], eng=POOLC)
                    TT(Bt[:, c, :], tmpb[:], eneg[:], ALU.mult, r=["tmpb", "eneg"], w=[("Bt", c)], eng=POOLC)
                    arv = AR[:, c, :, :, :]
                    STT(arv[:, :, 0, :], kkn[:].rearrange("p (j t) -> p j t", t=C), -1.0,
                        eexc[:].rearrange("p (j t) -> p j t", t=C), ALU.mult, ALU.mult, r=["kkn", "eexc"], w=[(kAR, c)])
                    TT(arv[:, :, 1, :], rr[:].rearrange("p (j t) -> p j t", t=C), epos[:].rearrange("p (j t) -> p j t", t=C),
                       ALU.mult, r=[("lrp", 0), "epos", (kAR, c)], w=[(kAR, c)])
                elif 5 <= si <= 6:
                    for f2, ch in enumerate((ca, cb)):
                        c = ch - 8
                        pi = lerp_ctr[0] % 2; lerp_ctr[0] += 1
                        lerp(pb2[f2], ch, tmpa[:], ["tmpa"], pi)
                        CP(vTb[:, c, :], tmpa[:], r=["tmpa"], w=[(kvTb, c)], eng="act")
                elif 7 <= si <= 8:
                    for f2, ch in enumerate((ca, cb)):
                        c = ch - 14
                        ACT(ug[:, c, :], ps[pb2[f2]][:, 0:TM], AF.Gelu, r=[("ps", pb2[f2])], w=[("ug", c)])
                else:
                    for f2, ch in enumerate((ca, cb)):
                        c = ch - 18
                        ACT(vf[:, c, :], ps[pb2[f2]][:, 0:TM], AF.Gelu, r=[("ps", pb2[f2])], w=[("vf", c)])
                        CP(vfb[:, c, :], vf[:, c, :], r=[("vf", c)], w=[("vfb", c)])
                        ACT(vfq[:, c, :], vf[:, c, :], AF.Square, r=[("vf", c)], w=[("vfq", c)])

            S.hook = None
            run_deferred(10 ** 6)
            bk_dom[0] = None
            if ms < 2:
                return []
            S.tag = S.tag.split("/")[0] + "/ln_tr_gmlp"
            b1, b2 = bank(), bank()
            for c in range(4):
                MM(ps[b1][:, 0:TM], onesb[:], vfb[:, c, :], r=["onesb", ("vfb", c)], w=[("ps", b1)], start=(c == 0), stop=(c == 3))
            for c in range(4):
                MM(ps[b2][:, 0:TM], onesb[:], vfq[:, c, :], r=["onesb", ("vfq", c)], w=[("ps", b2)], start=(c == 0), stop=(c == 3))
            CP(lnm[:], ps[b1][:, 0:TM], r=[("ps", b1)], w=["lnm"], eng="act")
            S.add("act", lambda e: e.mul(out=lnm[:], in_=lnm[:], mul=1.0 / 512), r=["lnm"], w=["lnm"])
            TT(tmpa[:], lnm[:], lnm[:], ALU.mult, r=["lnm"], w=["tmpa"])
            STT(tmpb[:], ps[b2][:, 0:TM], 1.0 / 512, tmpa[:], ALU.mult, ALU.subtract, r=[("ps", b2), "tmpa"], w=["tmpb"])
            TS(tmpb[:], tmpb[:], 0.0, None, ALU.max, None, r=["tmpb"], w=["tmpb"])
            RSQRT(lnr[:], tmpb[:], r=["tmpb", "epsb"], w=["lnr"], bias=epsb[:, 1:2])
            for c in range(4):
                TT(tmpa[:], vf[:, c, :], lnm[:], ALU.subtract, r=[("vf", c), "lnm"], w=["tmpa"])
                TT(tmpa[:], tmpa[:], lnr[:], ALU.mult, r=["tmpa", "lnr"], w=["tmpa"])
                TS(vnb[:, c, :], tmpa[:], pvc(PV_LNG + c), pvc(PV_LNB + c), ALU.mult, ALU.add, r=["tmpa", "pv"], w=[("vnb", c)])
            for jj in range(NJJ):
                for src, sn, dst, nm, ev in ((Bt, "Bt", Btok, "Btok", "act"), (Kt, "Kt", Ktok, "Ktok", "dve"),
                                             (vTb, kvTb, Vtok, "Vtok", "act"), (vnb, "vnb", VNtok, "VNtok", "dve")):
                    b = bank()
                    pbf = ps[b][:].bitcast(BF16)
                    for c in range(4):
                        TR(pbf[:, c * 128:(c + 1) * 128], src[:, c, jj * 128:(jj + 1) * 128], idb[:],
                           r=[(sn, c), "idb"], w=[("ps", b)])
                    CP(dst[:, jj, :], pbf[:, 0:512], r=[("ps", b)], w=[(nm, jj)], eng=ev)
            for jj in range(NJJ):
                for c in range(4):
                    b = bank()
                    for half in range(2):
                        g = 2 * c + half
                        MM(ps[b][half * 64:(half + 1) * 64, 0:128], VNtok[:, jj, g * 64:(g + 1) * 64], wstb[:, g, :],
                           r=[("VNtok", jj), ("wstb", g)], w=[("ps", b)])
                    TT(mix_t[:, 0:128], ps[b][:, 0:128], sbb[:, c, :], ALU.add, r=[("ps", b), "sbb"], w=["mix_t"])
                    TT(cat[:, 4 + c, t0 + jj * 128:t0 + (jj + 1) * 128], mix_t[:, 0:128], ug[:, c, jj * 128:(jj + 1) * 128],
                       ALU.mult, r=["mix_t", ("ug", c)], w=[("cat", 4 + c)])

            if ms < 3:
                return []
            S.tag = S.tag.split("/")[0] + "/scores"
            for jj in range(NJJ):
                bq = [bank(), bank()]
                bl = [bank(), bank()]
                bp = [bank(), bank()]
                for h in range(8):
                    c, hpar = h // 2, h % 2
                    hp = hpar * 64
                    for jh in range(2):
                        j = 2 * jj + jh
                        op_ = slice(jh * 64, jh * 64 + 64)
                        fr = slice(hp, hp + 64)
                        ar2 = AR[fr, c, j, :, :].rearrange("p a t -> p (a t)")
                        bcol = Bt[fr, c, j * C:(j + 1) * C]
                        kcol = Kt[fr, c, j * C:(j + 1) * C]
                        MM(ps[bq[hpar]][op_, c * 128:(c + 1) * 128], bcol, ar2, r=[("Bt", c), (kAR, c)], w=[("ps", bq[hpar])])
                        MM(ps[bl[hpar]][op_, c * 128:(c + 1) * 128], kcol, ar2, r=[("Kt", c), (kAR, c)], w=[("ps", bl[hpar])])
                        MM(ps[bp[hpar]][op_, c * 64:(c + 1) * 64], AR[fr, c, j, 0, :], bcol, r=[("Bt", c), (kAR, c)], w=[("ps", bp[hpar])])
                mq3 = mqm[:].unsqueeze(1).broadcast_to([128, 4, 128])
                ms3 = msl[:].unsqueeze(1).broadcast_to([128, 4, 64])
                for hpar in range(2):
                    TT(QM[:, jj, hpar::2, :], ps[bq[hpar]][:].rearrange("p (h t) -> p h t", t=128), mq3, ALU.mult,
                       r=[("ps", bq[hpar]), "mqm"], w=[("QM", jj)])
                    TT(LM[:, jj, hpar::2, :], ps[bl[hpar]][:].rearrange("p (h t) -> p h t", t=128), mq3, ALU.mult,
                       r=[("ps", bl[hpar]), "mqm"], w=[("LM", jj)])
                    TT(Pk[0][:, jj, :].rearrange("p (h t) -> p h t", t=64)[:, hpar::2, :],
                       ps[bp[hpar]][:, 0:256].rearrange("p (h t) -> p h t", t=64), ms3, ALU.mult,
                       r=[("ps", bp[hpar]), "msl"], w=PkK(0, jj))
                ib3 = iblk[:].unsqueeze(1).broadcast_to([128, 8, 64])
                CP(Qk[0][:, jj, :].rearrange("p (h t) -> p h t", t=64), QM[:, jj, :, 0:64], r=[("QM", jj)], w=QkK(0, jj))
                TT(Xk[0][:, jj, :].rearrange("p (h t) -> p h t", t=64), QM[:, jj, :, 0:64], ib3, ALU.add,
                   r=[("QM", jj), "iblk"], w=XkK(0, jj))

            if ms < 4:
                return []
            S.tag = S.tag.split("/")[0] + "/neumann"
            def blk(tl, jj, h, jh):
                return tl[jh * 64:(jh + 1) * 64, jj, h * 64:(h + 1) * 64]

            for lev in range(1, 6):
                pi_, po_ = (lev - 1) % 2, lev % 2
                for jj in range(NJJ):
                    bP = [bank(), bank()]
                    for jh in range(2):
                        rows = slice(jh * 64, jh * 64 + 64)
                        for h in range(8):
                            MM(ps[bP[jh]][rows, h * 64:(h + 1) * 64], blk(Qk[pi_], jj, h, jh), blk(Pk[pi_], jj, h, jh),
                               r=[QkK(pi_, jj), PkK(pi_, jj)], w=[("ps", bP[jh])])
                    for jh in range(2):
                        rows = slice(jh * 64, jh * 64 + 64)
                        CP(Pk[po_][rows, jj, :], ps[bP[jh]][rows, :], r=[("ps", bP[jh])], w=PkK(po_, jj), eng="act")
                    if lev <= 4:
                        bQ = [bank(), bank()]
                        for jh in range(2):
                            rows = slice(jh * 64, jh * 64 + 64)
                            for h in range(8):
                                MM(ps[bQ[jh]][rows, h * 64:(h + 1) * 64], blk(Pk[pi_], jj, h, jh), blk(Qk[pi_], jj, h, jh),
                                   r=[QkK(pi_, jj), PkK(pi_, jj)], w=[("ps", bQ[jh])])
                        for jh in range(2):
                            rows = slice(jh * 64, jh * 64 + 64)
                            CP(Qk[po_][rows, jj, :], ps[bQ[jh]][rows, :], r=[("ps", bQ[jh])], w=QkK(po_, jj), eng="act")
                    bX = [bank(), bank()]
                    for jh in range(2):
                        rows = slice(jh * 64, jh * 64 + 64)
                        for h in range(8):
                            MM(ps[bX[jh]][rows, h * 64:(h + 1) * 64], blk(Pk[po_], jj, h, jh), blk(Xk[pi_], jj, h, jh),
                               r=[PkK(po_, jj), XkK(pi_, jj)], w=[("ps", bX[jh])])
                    for jh in range(2):
                        rows = slice(jh * 64, jh * 64 + 64)
                        TT(Xk[po_][rows, jj, :], ps[bX[jh]][rows, :], Xk[pi_][rows, jj, :], ALU.add,
                           r=[("ps", bX[jh]), XkK(pi_, jj)], w=XkK(po_, jj))
            XF = Xk[5 % 2]

            if ms < 5:
                return []
            S.tag = S.tag.split("/")[0] + "/gkv"
            for jj in range(NJJ):
                bG = [bank(), bank()]
                for jh in range(2):
                    rows = slice(jh * 64, jh * 64 + 64)
                    for h in range(8):
                        MM(ps[bG[jh]][rows, h * 64:(h + 1) * 64], LM[rows, jj, h, 0:64], Vtok[rows, jj, h * 64:(h + 1) * 64],
                           r=[("LM", jj), ("Vtok", jj)], w=[("ps", bG[jh])])
                for jh in range(2):
                    rows = slice(jh * 64, jh * 64 + 64)
                    CP(Gtok[rows, jj, :], ps[bG[jh]][rows, :], r=[("ps", bG[jh])], w=[("Gtok", jj)], eng="act")
            bKV = [bank(), bank()]
            for jh in range(2):
                rows = slice(jh * 64, jh * 64 + 64)
                for h in range(8):
                    c, hp = h // 2, (h % 2) * 64
                    for jj in range(NJJ):
                        col = (c * NJJ + jj) * 64
                        MM(ps[bKV[jh]][hp:hp + 64, col:col + 64], Ktok[rows, jj, h * 64:(h + 1) * 64],
                           Vtok[rows, jj, h * 64:(h + 1) * 64], r=[("Ktok", jj), ("Vtok", jj)], w=[("ps", bKV[jh])])
            for jh in range(2):
                CP(KVs[:, :, jh::2, :], ps[bKV[jh]][:, 0:4 * NJJ * 64].rearrange("p (c j n) -> p c j n", c=4, j=NJJ),
                   r=[("ps", bKV[jh])], w=[("KVs", jh)], eng="dve")

            if ms < 6:
                return []
            base_tag = S.tag.split("/")[0]

            def chain_step(j):
                S.tag = base_tag + "/chain"
                jj, jh = j // 2, j % 2
                rows = slice(jh * 64, jh * 64 + 64)
                TT(HK[:], Hst[:], KVs[:, :, j, :], ALU.add, r=["Hst", ("KVs", 0), ("KVs", 1)], w=["HK"])
                bZ = [bank(), bank()]
                bYa = [bank(), bank()]
                for h in range(8):
                    c, hpar = h // 2, h % 2
                    fr = slice(hpar * 64, hpar * 64 + 64)
                    MM(ps[bZ[hpar]][rows, c * 64:(c + 1) * 64], AR[fr, c, j, 0, :], Hb[fr, c, :], r=[(kAR, c), "Hb"], w=[("ps", bZ[hpar])])
                for h in range(8):
                    c, hpar = h // 2, h % 2
                    fr = slice(hpar * 64, hpar * 64 + 64)
                    MM(ps[bYa[hpar]][rows, c * 64:(c + 1) * 64], AR[fr, c, j, 1, :], Hb[fr, c, :], r=[(kAR, c), "Hb"], w=[("ps", bYa[hpar])])
                yield
                zg3 = ZGb[rows, jj, :].rearrange("p (h n) -> p h n", n=64)
                gt3 = Gtok[rows, jj, :].rearrange("p (h n) -> p h n", n=64)
                ya3 = Ya[rows, jj, :].rearrange("p (h n) -> p h n", n=64)
                for hpar in range(2):
                    TT(zg3[:, hpar::2, :], ps[bZ[hpar]][rows, 0:256].rearrange("p (h n) -> p h n", n=64), gt3[:, hpar::2, :], ALU.add,
                       r=[("ps", bZ[hpar]), ("Gtok", jj)], w=[("ZGb", j)])
                for hpar in range(2):
                    CP(ya3[:, hpar::2, :], ps[bYa[hpar]][rows, 0:256].rearrange("p (h n) -> p h n", n=64),
                       r=[("ps", bYa[hpar])], w=YaK(jj), eng="act")
                yield
                bU = bank()
                for h in range(8):
                    MM(ps[bU][rows, h * 64:(h + 1) * 64], XF[rows, jj, h * 64:(h + 1) * 64], ZGb[rows, jj, h * 64:(h + 1) * 64],
                       r=[XkK(5 % 2, jj), ("ZGb", j)], w=[("ps", bU)])
                yield
                CP(Usb[rows, jj, :], ps[bU][rows, :], r=[("ps", bU)], w=[("Usb", j)], eng="act")
                yield
                bH = bank()
                for h in range(8):
                    c, hp = h // 2, (h % 2) * 64
                    MM(ps[bH][hp:hp + 64, c * 64:(c + 1) * 64], Btok[rows, jj, h * 64:(h + 1) * 64], Usb[rows, jj, h * 64:(h + 1) * 64],
                       r=[("Btok", jj), ("Usb", j)], w=[("ps", bH)])
                bY = bank()
                for h in range(8):
                    MM(ps[bY][rows, h * 64:(h + 1) * 64], QM[rows, jj, h, 64:128], Usb[rows, jj, h * 64:(h + 1) * 64],
                       r=[("QM", jj), ("Usb", j)], w=[("ps", bY)], start=True, stop=False)
                    MM(ps[bY][rows, h * 64:(h + 1) * 64], LM[rows, jj, h, 64:128], Vtok[rows, jj, h * 64:(h + 1) * 64],
                       r=[("LM", jj), ("Vtok", jj)], w=[("ps", bY)], start=False, stop=True)
                yield
                TT(Ht[:].rearrange("p c n -> p (c n)"), ps[bH][:, 0:256], HK[:].rearrange("p c n -> p (c n)"), ALU.add,
                   r=[("ps", bH), "HK"], w=["Ht"])
                gb = gam[:, :, j:j + 1].broadcast_to([128, 4, 64])
                gkeys = [(kgam, 0), (kgam, 1), (kgam, 2), (kgam, 3)]
                TT(Hst[:], Ht[:], gb, ALU.mult, r=["Ht"] + gkeys, w=["Hst"])
                TT(Hb[:], Ht[:], gb, ALU.mult, r=["Ht"] + gkeys, w=["Hb"])
                TT(Yt[rows, jj, :], ps[bY][rows, :], Ya[rows, jj, :], ALU.add, r=[("ps", bY), YaK(jj)], w=YtK(jj))

            def gn_step(jj):
                S.tag = base_tag + "/gn"
                yk = YtK(jj)
                y3 = Yt[:, jj, :].rearrange("p (h n) -> p h n", n=64)
                S.add("dve", (lambda jj=jj, y3=y3: (lambda e: e.tensor_reduce(out=gst[:, jj, 0, :], in_=y3, axis=AX.X, op=ALU.add)))(),
                      r=yk, w=[("gst", jj)])
                TT(Ysq[:], Yt[:, jj, :], Yt[:, jj, :], ALU.mult, r=yk, w=["Ysq"])
                S.add("dve", (lambda jj=jj: (lambda e: e.tensor_reduce(out=gst[:, jj, 1, :], in_=Ysq[:].rearrange("p (h n) -> p h n", n=64),
                                                                       axis=AX.X, op=ALU.add)))(), r=["Ysq", ("gst", jj)], w=[("gst", jj)])
                gk = [("gst", jj)]
                TS(gst[:, jj, 2, :], gst[:, jj, 0, :], 1.0 / 64, None, ALU.mult, None, r=gk, w=gk)
                TT(gst[:, jj, 0, :], gst[:, jj, 2, :], gst[:, jj, 2, :], ALU.mult, r=gk, w=gk)
                STT(gst[:, jj, 3, :], gst[:, jj, 1, :], 1.0 / 64, gst[:, jj, 0, :], ALU.mult, ALU.subtract, r=gk, w=gk)
                TS(gst[:, jj, 3, :], gst[:, jj, 3, :], 0.0, None, ALU.max, None, r=gk, w=gk)
                RSQRT(gst[:, jj, 3, :], gst[:, jj, 3, :], r=gk + ["epsb"], w=gk, bias=epsb[:, 2:3])
                yield
                mb = gst[:, jj, 2, :].unsqueeze(2).broadcast_to([128, 8, 64])
                rb = gst[:, jj, 3, :].unsqueeze(2).broadcast_to([128, 8, 64])
                TT(y3, y3, mb, ALU.subtract, r=yk + gk, w=yk)
                TT(y3, y3, rb, ALU.mult, r=yk + gk, w=yk)
                bT = bank()
                for c in range(4):
                    TR(ps[bT][:, c * 128:(c + 1) * 128], Yt[:, jj, c * 128:(c + 1) * 128], co[:, CO_ID:CO_ID + 128], r=yk + ["co"], w=[("ps", bT)])
                yield
                for c in range(4):
                    yield
                    TS(ynT[:, 0:128], ps[bT][:, c * 128:(c + 1) * 128], pvc(PV_GNW + c), pvc(PV_GNB + c), ALU.mult, ALU.add,
                       r=[("ps", bT), "pv"], w=["ynT"])
                    bB = bank()
                    MM(ps[bB][:, 0:128], oneblk[:], rkr[:, c, jj * 128:(jj + 1) * 128], r=["oneblk", (krkr, c)], w=[("ps", bB)])
                    TT(mix_t[:, 0:128], ps[bB][:, 0:128], vTb[:, c, jj * 128:(jj + 1) * 128], ALU.mult, r=[("ps", bB), (kvTb, c)], w=["mix_t"])
                    TT(ynT[:, 0:128], ynT[:, 0:128], mix_t[:, 0:128], ALU.add, r=["ynT", "mix_t"], w=["ynT"])
                    TT(cat[:, c, t0 + jj * 128:t0 + (jj + 1) * 128], ynT[:, 0:128], Gg[:, c, jj * 128:(jj + 1) * 128], ALU.mult,
                       r=["ynT", (kGg, c)], w=[("cat", c)])

            steps = [chain_step(j) for j in range(NJ)] + [gn_step(jj) for jj in range(NJJ)]
            if defer:
                return steps
            for st_ in steps:
                for _ in st_:
                    pass
            return []

        def out_proj():
            sb_ = stats_begin()
            for m in range(KC):
                b = bank()
                for c in range(KC):
                    MM(ps[b][:], woutb[:, c, m * 128:(m + 1) * 128], cat[:, c, :], r=["woutb", ("cat", c)], w=[("ps", b)],
                       start=(c == 0), stop=(c == KC - 1))
                TT(xt[:, m, :], ps[b][:], xt[:, m, :], ALU.add, r=[("ps", b), ("xt", m)], w=[("xt", m)])
                stats_add(sb_, m)
            return sb_

        out_ops = []
        for it in range(ntiles):
            tsl = slice(it * T, (it + 1) * T)
            for hh in range(2):
                dma_in("pool", xt[:, hh * 4:(hh + 1) * 4, :], xT[:, hh * 4:(hh + 1) * 4, tsl], w=[("xt", kc) for kc in range(hh * 4, hh * 4 + 4)])
            sbk = None
            S.tag = f"t{it}.ffn1"
            if upto >= 1:
                rmsnorm_to(PV_N1, hn_out, "n1")
            if upto >= 2:
                sbk = ffn(0)
            S.tag = f"t{it}.mix"
            if it == 0:
                dump("x1", xt[:], r=[("xt", kc) for kc in range(KC)])
            if upto >= 3:
                rmsnorm_to(PV_NM, hn_out, "nm", b=sbk)
                sbk = None
                dfr = []
                for sub in range(T // TM):
                    S.tag = f"t{it}.mix{sub}"
                    dfr = mixer(sub, dfr, defer=(DEFER and sub == 0))
                if it == 0:
                    dump("cat", cat[:], r=[("cat", c) for c in range(KC)])
            S.tag = f"t{it}.outp"
            if upto >= 4:
                sbk = out_proj()
            if it == 0:
                dump("x2", xt[:], r=[("xt", kc) for kc in range(KC)])
            S.tag = f"t{it}.ffn2"
            if upto >= 5:
                rmsnorm_to(PV_N2, hn_out, "n2", b=sbk)
                sbk = ffn(1)
            rmsnorm_to(PV_NF, lambda kc: (ot_v[:, kc, :], ot_keys(kc)), "nf", b=sbk)
            for hh in range(2):
                rk = []
                for kc in range(hh * 4, hh * 4 + 4):
                    rk += ot_keys(kc)
                out_ops.append(S.add("pool", (lambda hh=hh, tsl=tsl: (lambda e: e.dma_start(out=outT[:, hh * 4:(hh + 1) * 4, tsl],
                                                                                         in_=ot_v[:, hh * 4:(hh + 1) * 4, :])))(),
                                     r=rk, w=[("outT", it, hh)], dma=True))
        fin_r = [("outT", it, hh) for it in range(ntiles) for hh in range(2)] + [("dbg", n) for n in dbg_d]
        S.add("pool", None, r=fin_r, w=["fin"])

        streams = S.emit(nc, sems, dsems, None)
        with nc.Block() as block:
            @block.tensor
            def _(e):
                _replay(e, "pe", streams["pe"], sems, dsems)

            @block.scalar
            def _(e):
                _replay(e, "act", streams["act"], sems, dsems)

            @block.vector
            def _(e):
                _replay(e, "dve", streams["dve"], sems, dsems)

            @block.gpsimd
            def _(e):
                _replay(e, "pool", streams["pool"], sems, dsems)

            @block.sync
            def _(e):
                _replay(e, "sp", streams["sp"], sems, dsems)
    nc._sched = S
    nc._sched_counts = dict(S.counts)
    nc._nops = len(S.ops)
    return nc


def _consts():
    co = np.zeros((128, CO_N), np.float32)
    p = np.arange(128)
    co[p, CO_ID + p] = 1.0
    co[:, CO_ONEB:CO_ONEB + 128] = (p[:, None] // 64 == p[None, :] // 64)
    r = (p % 64)[:, None]
    cidx = np.arange(64)[None, :]
    co[:, CO_MQM:CO_MQM + 64] = (cidx > r)
    co[:, CO_MQM + 64:CO_MQM + 128] = (cidx >= r)
    co[:, CO_MSL:CO_MSL + 64] = (r > cidx)
    co[:, CO_IB:CO_IB + 64] = (r == cidx)
    co[:, CO_MWS:CO_MWS + 128] = (p[:, None] <= p[None, :])
    return co


def _vec_cols(v):
    v = np.asarray(v, np.float32).reshape(-1, 128)
    return np.ascontiguousarray(v.T)


def _prep_shared(inp):
    f = lambda a: np.asarray(a, np.float32)
    pv = np.zeros((128, PV_N), np.float32)
    pv[:, PV_N1:PV_N1 + 8] = _vec_cols(f(inp["ffn1_norm"])[0])
    pv[:, PV_NM:PV_NM + 8] = _vec_cols(f(inp["mix_norm"])[0])
    pv[:, PV_N2:PV_N2 + 8] = _vec_cols(f(inp["ffn2_norm"])[0])
    pv[:, PV_NF:PV_NF + 8] = _vec_cols(f(inp["final_norm"]))
    mu = f(inp["mu_shift"])[0]
    pv[:, PV_MU:PV_MU + 14] = _vec_cols(mu)
    pv[:, PV_W0:PV_W0 + 4] = _vec_cols(f(inp["w0"])[0])
    pv[:, PV_A0:PV_A0 + 4] = _vec_cols(f(inp["a0"])[0])
    pv[:, PV_KK:PV_KK + 4] = _vec_cols(f(inp["k_k"])[0])
    pv[:, PV_KA:PV_KA + 4] = _vec_cols(f(inp["k_a"])[0])
    pv[:, PV_RK:PV_RK + 4] = _vec_cols(f(inp["r_k"])[0].reshape(-1))
    pv[:, PV_GNW:PV_GNW + 4] = _vec_cols(f(inp["gn_w"])[0])
    pv[:, PV_GNB:PV_GNB + 4] = _vec_cols(f(inp["gn_b"])[0])
    pv[:, PV_LNG:PV_LNG + 4] = _vec_cols(f(inp["sgu_ln_g"])[0])
    pv[:, PV_LNB:PV_LNB + 4] = _vec_cols(f(inp["sgu_ln_b"])[0])
    sh = {"pv": pv, "co": _consts()}
    for l, pre in enumerate(("ffn1", "ffn2")):
        w1 = f(inp[pre + "_w1"])[0].reshape(KC, 128, 11, 256)
        w3 = f(inp[pre + "_w3"])[0].reshape(KC, 128, 11, 256)
        w13 = np.stack([w1, w3], 0)
        sh[f"w13_{l}"] = np.ascontiguousarray(w13.transpose(3, 2, 0, 1, 4)).reshape(11, 128, 2 * KC * 256)
        w2 = f(inp[pre + "_w2"])[0].reshape(FC, 128, KC, 128)
        sh[f"w2_{l}"] = np.ascontiguousarray(w2.transpose(2, 1, 0, 3)).reshape(8, 128, FC * 128)
    win = f(inp["w_in"])[0].reshape(KC, 128, 22, 128)
    slabs = []
    for (ca, cb) in WIN_SLABS:
        sl = np.stack([win[:, :, ca, :], win[:, :, cb, :]], 2)
        slabs.append(sl.transpose(1, 0, 2, 3).reshape(128, KC * 256))
    sh["win"] = np.ascontiguousarray(np.stack(slabs, 0))
    wout = f(inp["w_out"])[0].reshape(KC, 128, 1024)
    sh["wout"] = np.ascontiguousarray(wout.transpose(1, 0, 2)).reshape(128, KC * 1024)
    sh["wup"] = np.ascontiguousarray(f(inp["w_lora_up"])[0])
    sh["aup"] = np.ascontiguousarray(f(inp["a_lora_up"])[0])
    sh["gup"] = np.ascontiguousarray(f(inp["g_lora_up"])[0])
    ws = f(inp["sgu_w"])[0]
    sh["wst"] = np.ascontiguousarray(ws.transpose(2, 0, 1)).reshape(128, 8 * 128)
    sbv = f(inp["sgu_b"])[0].reshape(4, 2, 128).transpose(1, 0, 2)
    sh["sbb"] = np.ascontiguousarray(np.repeat(sbv[:, None], 64, axis=1)).reshape(128, 4 * 128)
    return sh


def _x_layout(xb):
    return np.ascontiguousarray(xb.T.reshape(KC, 128, SEQ).transpose(1, 0, 2))


def _out_unlayout(o):
    return np.ascontiguousarray(o.transpose(1, 0, 2).reshape(D, SEQ).T)


_NC_CACHE = {}


def kernel(**inputs):
    x = np.asarray(inputs["x"], np.float32)
    sh = _prep_shared(inputs)
    if "nc" not in _NC_CACHE:
        _NC_CACHE["nc"] = build_nc()
    nc = _NC_CACHE["nc"]
    in_maps = []
    for b in range(NB):
        m = dict(sh)
        m["xT"] = _x_layout(x[b])
        in_maps.append(m)
    res = run_bass_kernel_spmd(nc, in_maps, core_ids=list(range(NB)))
    out = np.stack([_out_unlayout(np.asarray(res.results[b]["outT"])) for b in range(NB)], 0)
    return out.astype(np.float32)
```

```python
import numpy as np
import concourse.bass as bass
import concourse.mybir as mybir
from concourse.bass_utils import run_bass_kernel_spmd

F32 = mybir.dt.float32
BF16 = mybir.dt.bfloat16
AF = mybir.ActivationFunctionType
ALU = mybir.AluOpType
AX = mybir.AxisListType

D = 1024
SEQ = 4096
NB = 8
DFF = 2816
KC = 8
FC = 22
T = 512
TM = 256
NT = SEQ // T
C = 64
NH = 8
NWB = 2
DEFER = False
INC_STATS = False
STRICT_SAME_ENGINE = True
HOOK_K = 5
POOLC = "dve"
RMS_EPS = 1e-6
GN_EPS = 64e-5
LN_EPS = 1e-5
EM05 = float(np.exp(-0.5))

PV_N1, PV_NM, PV_N2, PV_NF = 0, 8, 16, 24
PV_MU = 32
PV_W0 = 46
PV_A0 = 50
PV_KK = 54
PV_KA = 58
PV_RK = 62
PV_GNW = 66
PV_GNB = 70
PV_LNG = 74
PV_LNB = 78
PV_N = 82
CO_ID = 0
CO_ONEB = 128
CO_MQM = 256
CO_MSL = 384
CO_IB = 448
CO_MWS = 512
CO_N = 640

WIN_SLABS = [(12, 13), (0, 4), (1, 5), (2, 6), (3, 7), (8, 9), (10, 11), (14, 15), (16, 17), (18, 19), (20, 21)]


def _flat(keys):
    out = []
    for k in keys:
        if isinstance(k, list):
            out.extend(_flat(k))
        else:
            out.append(k)
    return out


class _Op:
    __slots__ = ("eng", "fn", "deps", "dma", "needs_inc", "ticket", "clock", "sem", "semval", "idx", "tag")


class Sched:
    ENGS = ("pe", "act", "dve", "pool", "sp")
    NDMA = {"pool": 8, "sp": 8, "act": 2}

    def __init__(self):
        self.ops = []
        self.last_w = {}
        self.readers = {}
        self.dma_hist = {e: [] for e in self.NDMA}
        self.tag = ""
        self.hook = None
        self.last_pe = None
        self.last_pe_cls = None
        self._in_hook = False
        self._hcnt = 0

    def add(self, eng, fn, r=(), w=(), dma=False, pcls=None):
        op = _Op()
        op.eng, op.fn, op.dma = eng, fn, dma
        op.needs_inc = False
        op.tag = self.tag
        op.idx = len(self.ops)
        r = _flat(r)
        w = _flat(w)
        deps = {}
        for k in r:
            d = self.last_w.get(k)
            if d is not None:
                deps[d.idx] = (d, True)
        for k in w:
            d = self.last_w.get(k)
            if d is not None and d.idx not in deps:
                deps[d.idx] = (d, False)
            lastr = {}
            for d in self.readers.get(k, ()):
                if d.dma:
                    if d.idx not in deps:
                        deps[d.idx] = (d, False)
                else:
                    lastr[d.eng] = d
            for d in lastr.values():
                if d.idx not in deps:
                    deps[d.idx] = (d, False)
        dl = []
        for d, raw in deps.values():
            if d.dma:
                dl.append(d)
                continue
            if d.eng == eng and not dma:
                if eng == "pe":
                    continue
                if not raw and not STRICT_SAME_ENGINE:
                    continue
            dl.append(d)
        if eng == "pe" and fn is not None:
            if self.last_pe is not None and pcls != self.last_pe_cls:
                dl.append(self.last_pe)
            self.last_pe, self.last_pe_cls = op, pcls
        if dma:
            h = self.dma_hist[eng]
            n = self.NDMA[eng]
            op.sem = len(h) % n
            op.semval = 16 * (len(h) // n + 1)
            if len(h) >= n:
                dl.append(h[len(h) - n])
            h.append(op)
        for d in dl:
            d.needs_inc = True
        op.deps = dl
        for k in r:
            self.readers.setdefault(k, []).append(op)
        for k in w:
            self.last_w[k] = op
            self.readers[k] = []
        self.ops.append(op)
        if self.hook is not None and not self._in_hook and eng != "pe":
            self._hcnt += 1
            if self._hcnt % HOOK_K == 0:
                self._in_hook = True
                self.hook()
                self._in_hook = False
        return op

    def emit(self, nc, sems, dsems, engines):
        cnt = {e: 0 for e in self.ENGS}
        for op in self.ops:
            if op.dma:
                op.needs_inc = True
            elif op.needs_inc:
                cnt[op.eng] += 1
                op.ticket = cnt[op.eng]
        known = {e: {} for e in self.ENGS}
        streams = {e: [] for e in self.ENGS}
        for op in self.ops:
            kn = known[op.eng]
            waits = []
            for d in op.deps:
                key = ("d", d.eng, d.sem) if d.dma else d.eng
                val = d.semval if d.dma else d.ticket
                if kn.get(key, 0) >= val:
                    continue
                waits.append((key, val))
                for k2, v2 in d.clock.items():
                    if kn.get(k2, 0) < v2:
                        kn[k2] = v2
            wm = {}
            for k, v in waits:
                if wm.get(k, 0) < v:
                    wm[k] = v
            if op.needs_inc:
                clk = dict(kn)
                if op.dma:
                    clk[("d", op.eng, op.sem)] = op.semval
                else:
                    clk[op.eng] = op.ticket
                op.clock = clk
            streams[op.eng].append((op, list(wm.items())))
        self.counts = cnt
        return streams


def _replay(e, eng_name, stream, sems, dsems):
    for op, waits in stream:
        for key, val in waits:
            if isinstance(key, tuple):
                e.wait_ge(dsems[key[1]][key[2]], val)
            else:
                e.wait_ge(sems[key], val)
        if op.fn is None:
            continue
        ins = op.fn(e)
        if op.dma:
            ins.then_inc(dsems[op.eng][op.sem], 16)
        elif op.needs_inc:
            ins.then_inc(sems[op.eng], 1)


def build_nc(dbg=None, ntiles=NT, upto=9, ms=9):
    nc = bass.Bass("TRN2", target_bir_lowering=False)
    dram = {}

    def din(name, shape):
        dram[name] = nc.dram_tensor(name, list(shape), F32, kind="ExternalInput").ap()
        return dram[name]

    xT = din("xT", [128, KC, SEQ])
    pv_d = din("pv", [128, PV_N])
    co_d = din("co", [128, CO_N])
    w13_d = [din(f"w13_{l}", [11, 128, 2 * KC * 256]) for l in range(2)]
    w2_d = [din(f"w2_{l}", [8, 128, FC * 128]) for l in range(2)]
    win_d = din("win", [11, 128, KC * 256])
    wout_d = din("wout", [128, KC * 1024])
    wup_d = din("wup", [64, 512])
    aup_d = din("aup", [64, 512])
    gup_d = din("gup", [128, 512])
    wst_d = din("wst", [128, 8 * 128])
    sbb_d = din("sbb", [128, 4 * 128])
    outT = nc.dram_tensor("outT", [128, KC, SEQ], F32, kind="ExternalOutput").ap()
    w13_s = [nc.dram_tensor(f"w13s_{l}", [11, 128, 2 * KC * 256], BF16, kind="Internal").ap() for l in range(2)]
    w2_s = [nc.dram_tensor(f"w2s_{l}", [8, 128, FC * 128], BF16, kind="Internal").ap() for l in range(2)]
    win_s = nc.dram_tensor("wins", [11, 128, KC * 256], BF16, kind="Internal").ap()
    dbg_d = {}
    if dbg:
        for name, shape in dbg.items():
            dbg_d[name] = nc.dram_tensor("dbg_" + name, list(shape), F32, kind="ExternalOutput").ap()

    S = Sched()
    import contextlib
    es = contextlib.ExitStack()
    with es:
        def sb(name, shape, dt):
            return es.enter_context(nc.sbuf_tensor("s_" + name, list(shape), dt))

        xt = sb("xt", [128, KC, T], F32)
        hn = sb("hn", [128, KC, T], BF16)
        scr = sb("scr", [128, FC * T], BF16)
        w13b = [sb(f"w13b{i}", [128, 2, KC, 256], BF16) for i in range(NWB)]
        w2b = [sb(f"w2b{i}", [128, FC, 128], BF16) for i in range(NWB)]
        woutb = sb("woutb", [128, KC, 1024], BF16)
        wupb = sb("wupb", [64, 512], BF16)
        aupb = sb("aupb", [128, 512], BF16)
        gupb = sb("gupb", [128, 512], BF16)
        wstb = sb("wstb", [128, 8, 128], BF16)
        sbb = sb("sbb", [128, 4, 128], F32)
        pv = sb("pv", [128, PV_N], F32)
        pv2 = sb("pv2", [128, 32], F32)
        co = sb("co", [128, CO_N], F32)
        idb = sb("idb", [128, 128], BF16)
        onesb = sb("onesb", [128, 128], BF16)
        oneblk = sb("oneblk", [128, 128], BF16)
        mqm = sb("mqm", [128, 128], BF16)
        msl = sb("msl", [128, 64], BF16)
        iblk = sb("iblk", [128, 64], BF16)
        onesf = sb("onesf", [128, 64], F32)
        epsb = sb("epsb", [128, 4], F32)
        sq = sb("sq", [128, 2, T], BF16)
        carry = sb("carry", [128, 2, 14], F32)
        Hst = sb("Hst", [128, 4, 64], F32)
        Hb = sb("Hb", [128, 4, 64], BF16)
        HK = sb("HK", [128, 4, 64], F32)
        Ht = sb("Ht", [128, 4, 64], F32)
        praw = [sb(f"praw{i}", [128, TM + 1], F32) for i in range(2)]
        dtmp = [sb(f"dtmp{i}", [128, TM], F32) for i in range(2)]
        tanhw = sb("tanhw", [64, TM], BF16)
        palow = sb("palow", [128, TM], BF16)
        sigg = sb("sigg", [128, TM], BF16)
        Gg2 = [sb(f"Gg{i}", [128, 4, TM], BF16) for i in range(2)]
        lrp = [sb(f"lrp{i}", [128, TM], F32) for i in range(2)]
        ldt = sb("ldt", [128, TM], F32)
        clt = sb("clt", [128, TM], F32)
        cle = sb("cle", [128, TM], F32)
        epos = sb("epos", [128, TM], F32)
        eneg = sb("eneg", [128, TM], F32)
        eexc = sb("eexc", [128, TM], F32)
        alp = sb("alp", [128, TM], F32)
        kkt = sb("kkt", [128, TM], F32)
        kksq = sb("kksq", [128, TM], BF16)
        kkn = sb("kkn", [128, TM], F32)
        tmpa = sb("tmpa", [128, TM], F32)
        tmpb = sb("tmpb", [128, TM], F32)
        gam2 = [sb(f"gam{i}", [128, 4, TM // C], F32) for i in range(2)]
        AR2 = [sb(f"AR{i}", [128, 4, TM // C, 2, C], BF16) for i in range(2)]
        Bt = sb("Bt", [128, 4, TM], BF16)
        Kt = sb("Kt", [128, 4, TM], BF16)
        vTb2 = [sb(f"vTb{i}", [128, 4, TM], BF16) for i in range(2)]
        rkr2 = [sb(f"rkr{i}", [128, 4, TM], BF16) for i in range(2)]
        ug = sb("ug", [128, 4, TM], BF16)
        vf = sb("vf", [128, 4, TM], F32)
        vfb = sb("vfb", [128, 4, TM], BF16)
        vfq = sb("vfq", [128, 4, TM], BF16)
        vnb = sb("vnb", [128, 4, TM], BF16)
        lnm = sb("lnm", [128, TM], F32)
        lnr = sb("lnr", [128, TM], F32)
        NJJ = TM // 128
        NJ = TM // C
        Btok = sb("Btok", [128, NJJ, 512], BF16)
        Ktok = sb("Ktok", [128, NJJ, 512], BF16)
        Vtok = sb("Vtok", [128, NJJ, 512], BF16)
        VNtok = sb("VNtok", [128, NJJ, 512], BF16)
        QM = sb("QM", [128, NJJ, 8, 128], BF16)
        LM = sb("LM", [128, NJJ, 8, 128], BF16)
        Gtok = sb("Gtok", [128, NJJ, 512], F32)
        KVs = sb("KVs", [128, 4, NJ, 64], F32)
        Usb = sb("Usb", [128, NJJ, 512], BF16)
        ZGb = sb("ZGb", [128, NJJ, 512], BF16)
        Ysq = sb("Ysq", [128, 512], F32)
        gst = sb("gst", [128, NJJ, 4, 8], F32)
        ynT = sb("ynT", [128, TM], F32)
        mix_t = sb("mix_t", [128, TM], F32)
        cat = sb("cat", [128, KC, T], BF16)

        ps = [es.enter_context(nc.psum_tensor(f"ps{i}", [128, 512], F32)) for i in range(8)]
        sems = {e: es.enter_context(nc.semaphore(f"s_{e}")) for e in Sched.ENGS}
        dsems = {e: [es.enter_context(nc.semaphore(f"d_{e}{i}")) for i in range(n)] for e, n in Sched.NDMA.items()}

        bk = [0, 0, 0]
        bk_dom = [None]

        bk_res = set()

        def bank():
            d = bk_dom[0]
            if d is None:
                i = bk[0]
                while i in bk_res:
                    i = (i + 1) % 8
                bk[0] = (i + 1) % 8
                return i
            i = bk[1 + d]
            bk[1 + d] = (i + 1) % 4
            return 4 * d + i

        act_v = scr[:].rearrange("p (f t) -> p f t", t=T)
        ot_v = scr[:, 0:KC * T * 2].bitcast(F32).rearrange("p (k t) -> p k t", t=T)

        def scrk(base, n):
            return [("scr", base + i) for i in range(n)]

        def scr_bf(off_bytes, nbytes):
            return scr[:, off_bytes // 2:(off_bytes + nbytes) // 2]

        Pk = [scr_bf(2048 * i, 2048).rearrange("p (j t) -> p j t", t=512) for i in range(2)]
        Qk = [scr_bf(4096 + 2048 * i, 2048).rearrange("p (j t) -> p j t", t=512) for i in range(2)]
        Xk = [scr_bf(8192 + 2048 * i, 2048).rearrange("p (j t) -> p j t", t=512) for i in range(2)]
        Ya = scr_bf(12288, 4096).bitcast(F32).rearrange("p (j t) -> p j t", t=512)
        Yt = scr_bf(16384, 4096).bitcast(F32).rearrange("p (j t) -> p j t", t=512)
        wstf = scr_bf(0, 4096).bitcast(F32).rearrange("p (g t) -> p g t", t=128)
        PkK = lambda i, jj: scrk(0 + 4 * i + 2 * jj, 2)
        QkK = lambda i, jj: scrk(8 + 4 * i + 2 * jj, 2)
        XkK = lambda i, jj: scrk(16 + 4 * i + 2 * jj, 2)
        YaK = lambda jj: scrk(24 + 4 * jj, 4)
        YtK = lambda jj: scrk(32 + 4 * jj, 4)
        wstfK = scrk(0, 8)
        rstd = Ysq
        silu_t = [Gtok[:, i, :] for i in range(2)]

        def act_keys(fc):
            return [("scr", 2 * fc), ("scr", 2 * fc + 1)]

        def ot_keys(kc):
            return [("scr", 4 * kc + i) for i in range(4)]

        def pvc(col, n=1, rows=slice(0, 128)):
            return pv[rows, col:col + n]

        def dma_in(eng, out_ap, in_ap, w, r=(), cast=False):
            if cast:
                S.add(eng, lambda e: e.dma_start(out=out_ap, in_=in_ap, max_dma_last_dim=8192), r=r, w=w, dma=True)
            else:
                S.add(eng, lambda e: e.dma_start(out=out_ap, in_=in_ap), r=r, w=w, dma=True)

        wseen = set()

        def load_w(sb2d, d32, dsc, key, skey, split):
            if skey not in wseen:
                wseen.add(skey)
                dma_in("pool", sb2d.rearrange("p (a b) -> p a b", b=split), d32.rearrange("p (a b) -> p a b", b=split),
                       w=[key], cast=True)
                S.add("sp", lambda e: e.dma_start(out=dsc, in_=sb2d), r=[key], w=[skey], dma=True)
            else:
                S.add("sp", lambda e: e.dma_start(out=sb2d, in_=dsc), r=[skey], w=[key], dma=True)

        def _psz(ap):
            v = ap.partition_size
            v = v() if callable(v) else v
            return 128 if v > 64 else (64 if v > 32 else 32)

        def MM(out_ap, lhsT, rhs, r, w, start=True, stop=True):
            S.add("pe", lambda e: e.matmul(out_ap, lhsT=lhsT, rhs=rhs, start=start, stop=stop), r=r, w=w,
                  pcls=(_psz(lhsT), _psz(out_ap), "mm"))

        def TR(out_ap, in_ap, ident, r, w):
            S.add("pe", lambda e: e.transpose(out_ap, in_ap, ident), r=r, w=w, pcls=(_psz(in_ap), _psz(out_ap), "mm"))

        def ACT(out_ap, in_ap, func, r, w, bias=None, scale=None):
            kw = {}
            if bias is not None:
                kw["bias"] = bias
            if scale is not None:
                kw["scale"] = scale
            S.add("act", lambda e: e.activation(out=out_ap, in_=in_ap, func=func, **kw), r=r, w=w)

        def TT(out_ap, a, b, op, r, w, eng="dve"):
            S.add(eng, lambda e: e.tensor_tensor(out=out_ap, in0=a, in1=b, op=op), r=r, w=w)

        def TS(out_ap, a, s1, s2, op0, op1, r, w, eng="dve"):
            if op1 is None:
                S.add(eng, lambda e: e.tensor_scalar(out=out_ap, in0=a, scalar1=s1, scalar2=None, op0=op0), r=r, w=w)
            else:
                S.add(eng, lambda e: e.tensor_scalar(out=out_ap, in0=a, scalar1=s1, scalar2=s2, op0=op0, op1=op1), r=r, w=w)

        def STT(out_ap, a, s, b, op0, op1, r, w):
            S.add("dve", lambda e: e.scalar_tensor_tensor(out=out_ap, in0=a, scalar=s, in1=b, op0=op0, op1=op1), r=r, w=w)

        def CP(out_ap, in_ap, r, w, eng="dve"):
            if eng == "act":
                S.add("act", lambda e: e.copy(out=out_ap, in_=in_ap), r=r, w=w)
            else:
                S.add(eng, lambda e: e.tensor_copy(out=out_ap, in_=in_ap), r=r, w=w)

        def RECIP(out_ap, in_ap, r, w):
            S.add("dve", lambda e: e.reciprocal(out=out_ap, in_=in_ap), r=r, w=w)

        def MEMSET(ap, val, w, eng="dve"):
            S.add(eng, lambda e: e.memset(ap, val), r=(), w=w)

        def dump(name, src_ap, r):
            if dbg and name in dbg_d:
                dst = dbg_d[name]
                if src_ap.dtype != F32:
                    S.add("pool", lambda e: e.dma_start(out=dst, in_=src_ap, max_dma_last_dim=2048), r=r, w=[("dbg", name)], dma=True)
                else:
                    S.add("sp", lambda e: e.dma_start(out=dst, in_=src_ap), r=r, w=[("dbg", name)], dma=True)

        dma_in("sp", pv[:], pv_d[:, :], w=["pv"])
        dma_in("sp", co[:], co_d[:, :], w=["co"])
        dma_in("sp", sbb[:].rearrange("p c t -> p (c t)"), sbb_d[:, :], w=["sbb"])
        dma_in("sp", wstf, wst_d.rearrange("p (g t) -> p g t", t=128), w=wstfK)
        dma_in("pool", woutb[:].rearrange("p k m -> p (k m)").rearrange("p (a b) -> p a b", b=2048),
               wout_d.rearrange("p (a b) -> p a b", b=2048), w=["woutb"], cast=True)
        dma_in("pool", wupb[:], wup_d[:, :], w=["wupb"], cast=True)
        dma_in("pool", aupb[64:128, :], aup_d[:, :], w=["aupb"], cast=True)
        dma_in("pool", gupb[:], gup_d[:, :], w=["gupb"], cast=True)
        CP(idb[:], co[:, CO_ID:CO_ID + 128], r=["co"], w=["idb"])
        CP(oneblk[:], co[:, CO_ONEB:CO_ONEB + 128], r=["co"], w=["oneblk"])
        CP(mqm[:], co[:, CO_MQM:CO_MQM + 128], r=["co"], w=["mqm"])
        CP(msl[:], co[:, CO_MSL:CO_MSL + 64], r=["co"], w=["msl"])
        CP(iblk[:], co[:, CO_IB:CO_IB + 64], r=["co"], w=["iblk"])
        MEMSET(onesb[:], 1.0, w=["onesb"])
        MEMSET(onesf[:], 1.0, w=["onesf"])
        MEMSET(epsb[:, 0:1], RMS_EPS, w=["epsb"])
        MEMSET(epsb[:, 1:2], LN_EPS, w=["epsb"])
        MEMSET(epsb[:, 2:3], GN_EPS, w=["epsb"])
        MEMSET(epsb[:, 3:4], 0.0, w=["epsb"])
        MEMSET(carry[:].rearrange("p a c -> p (a c)"), 0.0, w=[("carry", a_, c_) for a_ in range(2) for c_ in range(14)])
        MEMSET(Hst[:], 0.0, w=["Hst"])
        MEMSET(Hb[:], 0.0, w=["Hb"])
        for g in range(8):
            TT(wstb[:, g, :], wstf[:, g, :], co[:, CO_MWS:CO_MWS + 128], ALU.mult, r=[wstfK, "co"], w=[("wstb", g)])
        TS(pv2[:, 0:4], pv[:, PV_KA:PV_KA + 4], -1.0, 1.0, ALU.mult, ALU.add, r=["pv"], w=["pv2"])
        TS(pv2[:, 4:8], pv[:, PV_W0:PV_W0 + 4], -1.0, None, ALU.mult, None, r=["pv"], w=["pv2"])
        TS(pv2[:, 8:12], pv[:, PV_A0:PV_A0 + 4], -1.0, None, ALU.mult, None, r=["pv"], w=["pv2"])
        TS(pv2[:, 16:30], pv[:, PV_MU:PV_MU + 14], -1.0, 1.0, ALU.mult, ALU.add, r=["pv"], w=["pv2"])

        def SIGM(out_ap, in_ap, tmp_ap, r, w, tkey, negb=None, xs=1.0):
            if negb is None:
                ACT(tmp_ap, in_ap, AF.Exp, r=r, w=[tkey], scale=-xs)
            else:
                ACT(tmp_ap, in_ap, AF.Exp, r=r + ["pv2"], w=[tkey], scale=-xs, bias=negb)
            ACT(tmp_ap, tmp_ap, AF.Ln, r=[tkey], w=[tkey], bias=1.0)
            ACT(out_ap, tmp_ap, AF.Exp, r=[tkey], w=w, scale=-1.0)

        def RSQRT(out_ap, in_ap, r, w, bias=None, scale=None):
            ACT(out_ap, in_ap, AF.Ln, r=r, w=w, bias=bias, scale=scale)
            ACT(out_ap, out_ap, AF.Exp, r=w, w=w, scale=-0.5)

        def stats_begin():
            if not INC_STATS:
                return None
            b = bank()
            bk_res.add(b)
            return b

        def stats_add(b, kc):
            if b is None:
                return
            h = kc % 2
            ACT(sq[:, h, :], xt[:, kc, :], AF.Square, r=[("xt", kc)], w=[("sq", h)])
            MM(ps[b][:], onesb[:], sq[:, h, :], r=[("sq", h), "onesb"], w=[("ps", b)], start=(kc == 0), stop=(kc == KC - 1))
            if kc == KC - 1:
                bk_res.discard(b)

        def rmsnorm_to(gcol, out_fn, tag, b=None):
            if b is None:
                b = bank()
                for kc in range(KC):
                    h = kc % 2
                    ACT(sq[:, h, :], xt[:, kc, :], AF.Square, r=[("xt", kc)], w=[("sq", h)])
                    MM(ps[b][:], onesb[:], sq[:, h, :], r=[("sq", h), "onesb"], w=[("ps", b)], start=(kc == 0), stop=(kc == KC - 1))
            RSQRT(rstd[:], ps[b][:], r=[("ps", b), "epsb"], w=["Ysq"], bias=epsb[:, 0:1], scale=1.0 / D)
            for kc in range(KC):
                o, wk = out_fn(kc)
                STT(o, xt[:, kc, :], pvc(gcol + kc), rstd[:], ALU.mult, ALU.mult, r=[("xt", kc), "Ysq", "pv"], w=wk)

        def hn_out(kc):
            return hn[:, kc, :], [("hn", kc)]

        wslot = {"w13": 0, "w2": 0}

        def ffn(l):
            for s in range(11):
                bi = wslot["w13"] % NWB
                wslot["w13"] += 1
                wb = w13b[bi]
                load_w(wb[:].rearrange("p a k c -> p (a k c)"), w13_d[l][s], w13_s[l][s], ("w13b", bi), ("w13s", l, s), 2048)
                for f2 in range(2):
                    fc = 2 * s + f2
                    bg, bu = bank(), bank()
                    for which, bnk in ((0, bg), (1, bu)):
                        for kc in range(KC):
                            MM(ps[bnk][:], wb[:, which, kc, f2 * 128:(f2 + 1) * 128], hn[:, kc, :],
                               r=[("w13b", bi), ("hn", kc)], w=[("ps", bnk)], start=(kc == 0), stop=(kc == KC - 1))
                    st = silu_t[fc % 2]
                    ACT(st, ps[bg][:], AF.Silu, r=[("ps", bg)], w=[("Gtok", fc % 2)])
                    TT(act_v[:, fc, :], st, ps[bu][:], ALU.mult, r=[("Gtok", fc % 2), ("ps", bu)], w=act_keys(fc))
            sb_ = stats_begin()
            for m in range(KC):
                bi = wslot["w2"] % NWB
                wslot["w2"] += 1
                wb = w2b[bi]
                load_w(wb[:].rearrange("p f c -> p (f c)"), w2_d[l][m], w2_s[l][m], ("w2b", bi), ("w2s", l, m), 1408)
                b = bank()
                for fc in range(FC):
                    MM(ps[b][:], wb[:, fc, :], act_v[:, fc, :], r=[("w2b", bi)] + act_keys(fc), w=[("ps", b)],
                       start=(fc == 0), stop=(fc == FC - 1))
                STT(xt[:, m, :], ps[b][:], 0.5, xt[:, m, :], ALU.mult, ALU.add, r=[("ps", b), ("xt", m)], w=[("xt", m)])
                stats_add(sb_, m)
            return sb_

        mix_ctr = [0]

        def mixer(sub, deferred_in, defer):
            t0 = sub * TM
            pq = sub % 2
            Gg, gam, AR, vTb, rkr = Gg2[pq], gam2[pq], AR2[pq], vTb2[pq], rkr2[pq]
            kGg, kgam, kAR, kvTb, krkr = f"Gg{pq}", f"gam{pq}", f"AR{pq}", f"vTb{pq}", f"rkr{pq}"
            deferred_in = list(deferred_in)
            cpar = mix_ctr[0] % 2
            mix_ctr[0] += 1

            def run_deferred(n):
                old = S.tag
                bk_dom[0] = 1
                while n > 0 and deferred_in:
                    try:
                        next(deferred_in[0])
                        n -= 1
                    except StopIteration:
                        deferred_in.pop(0)
                bk_dom[0] = 0 if deferred_in else None
                S.tag = old

            if deferred_in:
                bk_dom[0] = 0
                S.hook = lambda: run_deferred(1)
            S.tag = S.tag.split("/")[0] + "/proj"
            hs = lambda kc: hn[:, kc, t0:t0 + TM]
            hk = [("hn", kc) for kc in range(KC)]

            def lerp(psb, ch_id, out_ap, out_keys, pi):
                pr = praw[pi]
                dt_ = dtmp[pi]
                CP(pr[:, 1:TM + 1], ps[psb][:, 0:TM], r=[("ps", psb)], w=[("prawb", pi)], eng="act")
                CP(pr[:, 0:1], carry[:, cpar, ch_id:ch_id + 1], r=[("carry", cpar, ch_id)], w=[("praw0", pi)])
                CP(carry[:, 1 - cpar, ch_id:ch_id + 1], pr[:, TM:TM + 1], r=[("prawb", pi)], w=[("carry", 1 - cpar, ch_id)])
                S.add("act", lambda e: e.activation(out=dt_[:], in_=ps[psb][:, 0:TM], func=AF.Copy, scale=pv2[:, 16 + ch_id:17 + ch_id]),
                      r=[("ps", psb), "pv2"], w=[("dtmp", pi)])
                STT(out_ap, pr[:, 0:TM], pvc(PV_MU + ch_id), dt_[:], ALU.mult, ALU.add,
                    r=[("prawb", pi), ("praw0", pi), ("dtmp", pi), "pv"], w=out_keys)

            lerp_ctr = [0]
            for si, (ca, cb) in enumerate(WIN_SLABS):
                bi = wslot["w13"] % NWB
                wslot["w13"] += 1
                wb = w13b[bi]
                wv = wb[:].rearrange("p a k c -> p (a k c)")[:, 0:KC * 256]
                load_w(wv, win_d[si], win_s[si], ("w13b", bi), ("wins", si), 2048)
                wv3 = wv.rearrange("p (k c) -> p k c", c=256)
                pb2 = []
                for f2 in range(2):
                    b = bank()
                    pb2.append(b)
                    for kc in range(KC):
                        MM(ps[b][:, 0:TM], wv3[:, kc, f2 * 128:(f2 + 1) * 128], hs(kc), r=[("w13b", bi), ("hn", kc)],
                           w=[("ps", b)], start=(kc == 0), stop=(kc == KC - 1))
                if si == 0:
                    pi = lerp_ctr[0] % 2; lerp_ctr[0] += 1
                    lerp(pb2[0], 12, tmpa[:], ["tmpa"], pi)
                    SIGM(kkt[0:64, :], tmpa[0:64, :], kkt[0:64, :], r=["tmpa"], w=["kkt"], tkey="kkt", xs=2.0)
                    TS(tanhw[:], kkt[0:64, :], 2.0, -1.0, ALU.mult, ALU.add, r=["kkt"], w=["tanhw"])
                    CP(palow[64:128, :], tmpa[64:128, :], r=["tmpa"], w=["palow"], eng="act")
                    pi = lerp_ctr[0] % 2; lerp_ctr[0] += 1
                    lerp(pb2[1], 13, tmpb[:], ["tmpb"], pi)
                    SIGM(sigg[:], tmpb[:], kkn[:], r=["tmpb"], w=["sigg"], tkey="kkn")
                    for c in range(4):
                        b = bank()
                        MM(ps[b][:, 0:TM], gupb[:, c * 128:(c + 1) * 128], sigg[:], r=["gupb", "sigg"], w=[("ps", b)])
                        CP(Gg[:, c, :], ps[b][:, 0:TM], r=[("ps", b)], w=[(kGg, c)], eng="act")
                elif 1 <= si <= 4:
                    c = si - 1
                    bz = bank()
                    MM(ps[bz][:, 0:TM], wupb[:, c * 128:(c + 1) * 128], tanhw[:], r=["wupb", "tanhw"], w=[("ps", bz)])
                    ba = bank()
                    MM(ps[ba][:, 0:TM], aupb[64:128, c * 128:(c + 1) * 128], palow[64:128, :], r=["aupb", "palow"], w=[("ps", ba)])
                    SIGM(ldt[:], ps[bz][:, 0:TM], ldt[:], r=[("ps", bz)], w=["ldt"], tkey="ldt", negb=pv2[:, 4 + c:5 + c])
                    SIGM(alp[:], ps[ba][:, 0:TM], alp[:], r=[("ps", ba)], w=["alp"], tkey="alp", negb=pv2[:, 8 + c:9 + c])
                    for j in range(NJ):
                        S.add("dve", (lambda j=j: (lambda e: e.tensor_tensor_scan(
                            out=clt[:, j * C:(j + 1) * C], data0=onesf[:, 0:C], data1=ldt[:, j * C:(j + 1) * C],
                            initial=0.0, op0=ALU.mult, op1=ALU.add)))(), r=["ldt", "onesf"], w=["clt"])
                    TT(cle[:], clt[:], ldt[:], ALU.subtract, r=["clt", "ldt"], w=["cle"])
                    ACT(epos[:], clt[:], AF.Exp, r=["clt"], w=["epos"], scale=-EM05)
                    ACT(eneg[:], clt[:], AF.Exp, r=["clt"], w=["eneg"], scale=EM05)
                    ACT(eexc[:], cle[:], AF.Exp, r=["cle"], w=["eexc"], scale=-EM05)
                    CP(gam[:, c, :], epos[:].rearrange("p (j t) -> p j t", t=C)[:, :, C - 1], r=["epos"], w=[(kgam, c)])
                    pi = lerp_ctr[0] % 2; lerp_ctr[0] += 1
                    lerp(pb2[0], ca, lrp[0][:], [("lrp", 0)], pi)
                    pi = lerp_ctr[0] % 2; lerp_ctr[0] += 1
                    lerp(pb2[1], cb, lrp[1][:], [("lrp", 1)], pi)
                    rr, kk_ = lrp[0], lrp[1]
                    S.add("act", (lambda c=c, kk_=kk_: (lambda e: e.activation(out=kkt[:], in_=kk_[:], func=AF.Copy, scale=pvc(PV_KK + c))))(),
                          r=[("lrp", 1), "pv"], w=["kkt"])
                    ACT(kksq[:], kkt[:], AF.Square, r=["kkt"], w=["kksq"])
                    bn = bank()
                    MM(ps[bn][:, 0:TM], oneblk[:], kksq[:], r=["oneblk", "kksq"], w=[("ps", bn)])
                    TS(kkn[:], ps[bn][:, 0:TM], 1e-24, None, ALU.max, None, r=[("ps", bn)], w=["kkn"])
                    RSQRT(kkn[:], kkn[:], r=["kkn"], w=["kkn"])
                    TT(kkn[:], kkn[:], kkt[:], ALU.mult, r=["kkn", "kkt"], w=["kkn"])
                    ACT(tmpa[:], alp[:], AF.Identity, r=["alp", "pv", "pv2"], w=["tmpa"], bias=pv2[:, c:c + 1], scale=pvc(PV_KA + c))
                    TT(tmpa[:], tmpa[:], kk_[:], ALU.mult, r=["tmpa", ("lrp", 1)], w=["tmpa"], eng=POOLC)
                    STT(rkr[:, c, :], rr[:], pvc(PV_RK + c), tmpa[:], ALU.mult, ALU.mult, r=[("lrp", 0), "tmpa", "pv"], w=[(krkr, c)])
                    TT(Kt[:, c, :], tmpa[:], eneg[:], ALU.mult, r=["tmpa", "eneg"], w=[("Kt", c)], eng=POOLC)
                    TT(tmpb[:], kkn[:], alp[:], ALU.mult, r=["kkn", "alp"], w=["tmpb"], eng=POOLC)
                    TT(Bt[:, c, :], tmpb[:], eneg[:], ALU.mult, r=["tmpb", "eneg"], w=[("Bt", c)], eng=POOLC)
                    arv = AR[:, c, :, :, :]
                    STT(arv[:, :, 0, :], kkn[:].rearrange("p (j t) -> p j t", t=C), -1.0,
                        eexc[:].rearrange("p (j t) -> p j t", t=C), ALU.mult, ALU.mult, r=["kkn", "eexc"], w=[(kAR, c)])
                    TT(arv[:, :, 1, :], rr[:].rearrange("p (j t) -> p j t", t=C), epos[:].rearrange("p (j t) -> p j t", t=C),
                       ALU.mult, r=[("lrp", 0), "epos", (kAR, c)], w=[(kAR, c)])
                elif 5 <= si <= 6:
                    for f2, ch in enumerate((ca, cb)):
                        c = ch - 8
                        pi = lerp_ctr[0] % 2; lerp_ctr[0] += 1
                        lerp(pb2[f2], ch, tmpa[:], ["tmpa"], pi)
                        CP(vTb[:, c, :], tmpa[:], r=["tmpa"], w=[(kvTb, c)], eng="act")
                elif 7 <= si <= 8:
                    for f2, ch in enumerate((ca, cb)):
                        c = ch - 14
                        ACT(ug[:, c, :], ps[pb2[f2]][:, 0:TM], AF.Gelu, r=[("ps", pb2[f2])], w=[("ug", c)])
                else:
                    for f2, ch in enumerate((ca, cb)):
                        c = ch - 18
                        ACT(vf[:, c, :], ps[pb2[f2]][:, 0:TM], AF.Gelu, r=[("ps", pb2[f2])], w=[("vf", c)])
                        CP(vfb[:, c, :], vf[:, c, :], r=[("vf", c)], w=[("vfb", c)])
                        ACT(vfq[:, c, :], vf[:, c, :], AF.Square, r=[("vf", c)], w=[("vfq", c)])

            S.hook = None
            run_deferred(10 ** 6)
            bk_dom[0] = None
            if ms < 2:
                return []
            S.tag = S.tag.split("/")[0] + "/ln_tr_gmlp"
            b1, b2 = bank(), bank()
            for c in range(4):
                MM(ps[b1][:, 0:TM], onesb[:], vfb[:, c, :], r=["onesb", ("vfb", c)], w=[("ps", b1)], start=(c == 0), stop=(c == 3))
            for c in range(4):
                MM(ps[b2][:, 0:TM], onesb[:], vfq[:, c, :], r=["onesb", ("vfq", c)], w=[("ps", b2)], start=(c == 0), stop=(c == 3))
            CP(lnm[:], ps[b1][:, 0:TM], r=[("ps", b1)], w=["lnm"], eng="act")
            S.add("act", lambda e: e.mul(out=lnm[:], in_=lnm[:], mul=1.0 / 512), r=["lnm"], w=["lnm"])
            TT(tmpa[:], lnm[:], lnm[:], ALU.mult, r=["lnm"], w=["tmpa"])
            STT(tmpb[:], ps[b2][:, 0:TM], 1.0 / 512, tmpa[:], ALU.mult, ALU.subtract, r=[("ps", b2), "tmpa"], w=["tmpb"])
            TS(tmpb[:], tmpb[:], 0.0, None, ALU.max, None, r=["tmpb"], w=["tmpb"])
            RSQRT(lnr[:], tmpb[:], r=["tmpb", "epsb"], w=["lnr"], bias=epsb[:, 1:2])
            for c in range(4):
                TT(tmpa[:], vf[:, c, :], lnm[:], ALU.subtract, r=[("vf", c), "lnm"], w=["tmpa"])
                TT(tmpa[:], tmpa[:], lnr[:], ALU.mult, r=["tmpa", "lnr"], w=["tmpa"])
                TS(vnb[:, c, :], tmpa[:], pvc(PV_LNG + c), pvc(PV_LNB + c), ALU.mult, ALU.add, r=["tmpa", "pv"], w=[("vnb", c)])
            for jj in range(NJJ):
                for src, sn, dst, nm, ev in ((Bt, "Bt", Btok, "Btok", "act"), (Kt, "Kt", Ktok, "Ktok", "dve"),
                                             (vTb, kvTb, Vtok, "Vtok", "act"), (vnb, "vnb", VNtok, "VNtok", "dve")):
                    b = bank()
                    pbf = ps[b][:].bitcast(BF16)
                    for c in range(4):
                        TR(pbf[:, c * 128:(c + 1) * 128], src[:, c, jj * 128:(jj + 1) * 128], idb[:],
                           r=[(sn, c), "idb"], w=[("ps", b)])
                    CP(dst[:, jj, :], pbf[:, 0:512], r=[("ps", b)], w=[(nm, jj)], eng=ev)
            for jj in range(NJJ):
                for c in range(4):
                    b = bank()
                    for half in range(2):
                        g = 2 * c + half
                        MM(ps[b][half * 64:(half + 1) * 64, 0:128], VNtok[:, jj, g * 64:(g + 1) * 64], wstb[:, g, :],
                           r=[("VNtok", jj), ("wstb", g)], w=[("ps", b)])
                    TT(mix_t[:, 0:128], ps[b][:, 0:128], sbb[:, c, :], ALU.add, r=[("ps", b), "sbb"], w=["mix_t"])
                    TT(cat[:, 4 + c, t0 + jj * 128:t0 + (jj + 1) * 128], mix_t[:, 0:128], ug[:, c, jj * 128:(jj + 1) * 128],
                       ALU.mult, r=["mix_t", ("ug", c)], w=[("cat", 4 + c)])

            if ms < 3:
                return []
            S.tag = S.tag.split("/")[0] + "/scores"
            for jj in range(NJJ):
                bq = [bank(), bank()]
                bl = [bank(), bank()]
                bp = [bank(), bank()]
                for h in range(8):
                    c, hpar = h // 2, h % 2
                    hp = hpar * 64
                    for jh in range(2):
                        j = 2 * jj + jh
                        op_ = slice(jh * 64, jh * 64 + 64)
                        fr = slice(hp, hp + 64)
                        ar2 = AR[fr, c, j, :, :].rearrange("p a t -> p (a t)")
                        bcol = Bt[fr, c, j * C:(j + 1) * C]
                        kcol = Kt[fr, c, j * C:(j + 1) * C]
                        MM(ps[bq[hpar]][op_, c * 128:(c + 1) * 128], bcol, ar2, r=[("Bt", c), (kAR, c)], w=[("ps", bq[hpar])])
                        MM(ps[bl[hpar]][op_, c * 128:(c + 1) * 128], kcol, ar2, r=[("Kt", c), (kAR, c)], w=[("ps", bl[hpar])])
                        MM(ps[bp[hpar]][op_, c * 64:(c + 1) * 64], AR[fr, c, j, 0, :], bcol, r=[("Bt", c), (kAR, c)], w=[("ps", bp[hpar])])
                mq3 = mqm[:].unsqueeze(1).broadcast_to([128, 4, 128])
                ms3 = msl[:].unsqueeze(1).broadcast_to([128, 4, 64])
                for hpar in range(2):
                    TT(QM[:, jj, hpar::2, :], ps[bq[hpar]][:].rearrange("p (h t) -> p h t", t=128), mq3, ALU.mult,
                       r=[("ps", bq[hpar]), "mqm"], w=[("QM", jj)])
                    TT(LM[:, jj, hpar::2, :], ps[bl[hpar]][:].rearrange("p (h t) -> p h t", t=128), mq3, ALU.mult,
                       r=[("ps", bl[hpar]), "mqm"], w=[("LM", jj)])
                    TT(Pk[0][:, jj, :].rearrange("p (h t) -> p h t", t=64)[:, hpar::2, :],
                       ps[bp[hpar]][:, 0:256].rearrange("p (h t) -> p h t", t=64), ms3, ALU.mult,
                       r=[("ps", bp[hpar]), "msl"], w=PkK(0, jj))
                ib3 = iblk[:].unsqueeze(1).broadcast_to([128, 8, 64])
                CP(Qk[0][:, jj, :].rearrange("p (h t) -> p h t", t=64), QM[:, jj, :, 0:64], r=[("QM", jj)], w=QkK(0, jj))
                TT(Xk[0][:, jj, :].rearrange("p (h t) -> p h t", t=64), QM[:, jj, :, 0:64], ib3, ALU.add,
                   r=[("QM", jj), "iblk"], w=XkK(0, jj))

            if ms < 4:
                return []
            S.tag = S.tag.split("/")[0] + "/neumann"
            def blk(tl, jj, h, jh):
                return tl[jh * 64:(jh + 1) * 64, jj, h * 64:(h + 1) * 64]

            for lev in range(1, 6):
                pi_, po_ = (lev - 1) % 2, lev % 2
                for jj in range(NJJ):
                    bP = [bank(), bank()]
                    for jh in range(2):
                        rows = slice(jh * 64, jh * 64 + 64)
                        for h in range(8):
                            MM(ps[bP[jh]][rows, h * 64:(h + 1) * 64], blk(Qk[pi_], jj, h, jh), blk(Pk[pi_], jj, h, jh),
                               r=[QkK(pi_, jj), PkK(pi_, jj)], w=[("ps", bP[jh])])
                    for jh in range(2):
                        rows = slice(jh * 64, jh * 64 + 64)
                        CP(Pk[po_][rows, jj, :], ps[bP[jh]][rows, :], r=[("ps", bP[jh])], w=PkK(po_, jj), eng="act")
                    if lev <= 4:
                        bQ = [bank(), bank()]
                        for jh in range(2):
                            rows = slice(jh * 64, jh * 64 + 64)
                            for h in range(8):
                                MM(ps[bQ[jh]][rows, h * 64:(h + 1) * 64], blk(Pk[pi_], jj, h, jh), blk(Qk[pi_], jj, h, jh),
                                   r=[QkK(pi_, jj), PkK(pi_, jj)], w=[("ps", bQ[jh])])
                        for jh in range(2):
                            rows = slice(jh * 64, jh * 64 + 64)
                            CP(Qk[po_][rows, jj, :], ps[bQ[jh]][rows, :], r=[("ps", bQ[jh])], w=QkK(po_, jj), eng="act")
                    bX = [bank(), bank()]
                    for jh in range(2):
                        rows = slice(jh * 64, jh * 64 + 64)
                        for h in range(8):
                            MM(ps[bX[jh]][rows, h * 64:(h + 1) * 64], blk(Pk[po_], jj, h, jh), blk(Xk[pi_], jj, h, jh),
                               r=[PkK(po_, jj), XkK(pi_, jj)], w=[("ps", bX[jh])])
                    for jh in range(2):
                        rows = slice(jh * 64, jh * 64 + 64)
                        TT(Xk[po_][rows, jj, :], ps[bX[jh]][rows, :], Xk[pi_][rows, jj, :], ALU.add,
                           r=[("ps", bX[jh]), XkK(pi_, jj)], w=XkK(po_, jj))
            XF = Xk[5 % 2]

            if ms < 5:
                return []
            S.tag = S.tag.split("/")[0] + "/gkv"
            for jj in range(NJJ):
                bG = [bank(), bank()]
                for jh in range(2):
                    rows = slice(jh * 64, jh * 64 + 64)
                    for h in range(8):
                        MM(ps[bG[jh]][rows, h * 64:(h + 1) * 64], LM[rows, jj, h, 0:64], Vtok[rows, jj, h * 64:(h + 1) * 64],
                           r=[("LM", jj), ("Vtok", jj)], w=[("ps", bG[jh])])
                for jh in range(2):
                    rows = slice(jh * 64, jh * 64 + 64)
                    CP(Gtok[rows, jj, :], ps[bG[jh]][rows, :], r=[("ps", bG[jh])], w=[("Gtok", jj)], eng="act")
            bKV = [bank(), bank()]
            for jh in range(2):
                rows = slice(jh * 64, jh * 64 + 64)
                for h in range(8):
                    c, hp = h // 2, (h % 2) * 64
                    for jj in range(NJJ):
                        col = (c * NJJ + jj) * 64
                        MM(ps[bKV[jh]][hp:hp + 64, col:col + 64], Ktok[rows, jj, h * 64:(h + 1) * 64],
                           Vtok[rows, jj, h * 64:(h + 1) * 64], r=[("Ktok", jj), ("Vtok", jj)], w=[("ps", bKV[jh])])
            for jh in range(2):
                CP(KVs[:, :, jh::2, :], ps[bKV[jh]][:, 0:4 * NJJ * 64].rearrange("p (c j n) -> p c j n", c=4, j=NJJ),
                   r=[("ps", bKV[jh])], w=[("KVs", jh)], eng="dve")

            if ms < 6:
                return []
            base_tag = S.tag.split("/")[0]

            def chain_step(j):
                S.tag = base_tag + "/chain"
                jj, jh = j // 2, j % 2
                rows = slice(jh * 64, jh * 64 + 64)
                TT(HK[:], Hst[:], KVs[:, :, j, :], ALU.add, r=["Hst", ("KVs", 0), ("KVs", 1)], w=["HK"])
                bZ = [bank(), bank()]
                bYa = [bank(), bank()]
                for h in range(8):
                    c, hpar = h // 2, h % 2
                    fr = slice(hpar * 64, hpar * 64 + 64)
                    MM(ps[bZ[hpar]][rows, c * 64:(c + 1) * 64], AR[fr, c, j, 0, :], Hb[fr, c, :], r=[(kAR, c), "Hb"], w=[("ps", bZ[hpar])])
                for h in range(8):
                    c, hpar = h // 2, h % 2
                    fr = slice(hpar * 64, hpar * 64 + 64)
                    MM(ps[bYa[hpar]][rows, c * 64:(c + 1) * 64], AR[fr, c, j, 1, :], Hb[fr, c, :], r=[(kAR, c), "Hb"], w=[("ps", bYa[hpar])])
                yield
                zg3 = ZGb[rows, jj, :].rearrange("p (h n) -> p h n", n=64)
                gt3 = Gtok[rows, jj, :].rearrange("p (h n) -> p h n", n=64)
                ya3 = Ya[rows, jj, :].rearrange("p (h n) -> p h n", n=64)
                for hpar in range(2):
                    TT(zg3[:, hpar::2, :], ps[bZ[hpar]][rows, 0:256].rearrange("p (h n) -> p h n", n=64), gt3[:, hpar::2, :], ALU.add,
                       r=[("ps", bZ[hpar]), ("Gtok", jj)], w=[("ZGb", j)])
                for hpar in range(2):
                    CP(ya3[:, hpar::2, :], ps[bYa[hpar]][rows, 0:256].rearrange("p (h n) -> p h n", n=64),
                       r=[("ps", bYa[hpar])], w=YaK(jj), eng="act")
                yield
                bU = bank()
                for h in range(8):
                    MM(ps[bU][rows, h * 64:(h + 1) * 64], XF[rows, jj, h * 64:(h + 1) * 64], ZGb[rows, jj, h * 64:(h + 1) * 64],
                       r=[XkK(5 % 2, jj), ("ZGb", j)], w=[("ps", bU)])
                yield
                CP(Usb[rows, jj, :], ps[bU][rows, :], r=[("ps", bU)], w=[("Usb", j)], eng="act")
                yield
                bH = bank()
                for h in range(8):
                    c, hp = h // 2, (h % 2) * 64
                    MM(ps[bH][hp:hp + 64, c * 64:(c + 1) * 64], Btok[rows, jj, h * 64:(h + 1) * 64], Usb[rows, jj, h * 64:(h + 1) * 64],
                       r=[("Btok", jj), ("Usb", j)], w=[("ps", bH)])
                bY = bank()
                for h in range(8):
                    MM(ps[bY][rows, h * 64:(h + 1) * 64], QM[rows, jj, h, 64:128], Usb[rows, jj, h * 64:(h + 1) * 64],
                       r=[("QM", jj), ("Usb", j)], w=[("ps", bY)], start=True, stop=False)
                    MM(ps[bY][rows, h * 64:(h + 1) * 64], LM[rows, jj, h, 64:128], Vtok[rows, jj, h * 64:(h + 1) * 64],
                       r=[("LM", jj), ("Vtok", jj)], w=[("ps", bY)], start=False, stop=True)
                yield
                TT(Ht[:].rearrange("p c n -> p (c n)"), ps[bH][:, 0:256], HK[:].rearrange("p c n -> p (c n)"), ALU.add,
                   r=[("ps", bH), "HK"], w=["Ht"])
                gb = gam[:, :, j:j + 1].broadcast_to([128, 4, 64])
                gkeys = [(kgam, 0), (kgam, 1), (kgam, 2), (kgam, 3)]
                TT(Hst[:], Ht[:], gb, ALU.mult, r=["Ht"] + gkeys, w=["Hst"])
                TT(Hb[:], Ht[:], gb, ALU.mult, r=["Ht"] + gkeys, w=["Hb"])
                TT(Yt[rows, jj, :], ps[bY][rows, :], Ya[rows, jj, :], ALU.add, r=[("ps", bY), YaK(jj)], w=YtK(jj))

            def gn_step(jj):
                S.tag = base_tag + "/gn"
                yk = YtK(jj)
                y3 = Yt[:, jj, :].rearrange("p (h n) -> p h n", n=64)
                S.add("dve", (lambda jj=jj, y3=y3: (lambda e: e.tensor_reduce(out=gst[:, jj, 0, :], in_=y3, axis=AX.X, op=ALU.add)))(),
                      r=yk, w=[("gst", jj)])
                TT(Ysq[:], Yt[:, jj, :], Yt[:, jj, :], ALU.mult, r=yk, w=["Ysq"])
                S.add("dve", (lambda jj=jj: (lambda e: e.tensor_reduce(out=gst[:, jj, 1, :], in_=Ysq[:].rearrange("p (h n) -> p h n", n=64),
                                                                       axis=AX.X, op=ALU.add)))(), r=["Ysq", ("gst", jj)], w=[("gst", jj)])
                gk = [("gst", jj)]
                TS(gst[:, jj, 2, :], gst[:, jj, 0, :], 1.0 / 64, None, ALU.mult, None, r=gk, w=gk)
                TT(gst[:, jj, 0, :], gst[:, jj, 2, :], gst[:, jj, 2, :], ALU.mult, r=gk, w=gk)
                STT(gst[:, jj, 3, :], gst[:, jj, 1, :], 1.0 / 64, gst[:, jj, 0, :], ALU.mult, ALU.subtract, r=gk, w=gk)
                TS(gst[:, jj, 3, :], gst[:, jj, 3, :], 0.0, None, ALU.max, None, r=gk, w=gk)
                RSQRT(gst[:, jj, 3, :], gst[:, jj, 3, :], r=gk + ["epsb"], w=gk, bias=epsb[:, 2:3])
                yield
                mb = gst[:, jj, 2, :].unsqueeze(2).broadcast_to([128, 8, 64])
                rb = gst[:, jj, 3, :].unsqueeze(2).broadcast_to([128, 8, 64])
                TT(y3, y3, mb, ALU.subtract, r=yk + gk, w=yk)
                TT(y3, y3, rb, ALU.mult, r=yk + gk, w=yk)
                bT = bank()
                for c in range(4):
                    TR(ps[bT][:, c * 128:(c + 1) * 128], Yt[:, jj, c * 128:(c + 1) * 128], co[:, CO_ID:CO_ID + 128], r=yk + ["co"], w=[("ps", bT)])
                yield
                for c in range(4):
                    yield
                    yv = ynT[:, (c % 2) * 128:(c % 2) * 128 + 128]
                    yk2 = ("ynT", c % 2)
                    ACT(yv, ps[bT][:, c * 128:(c + 1) * 128], AF.Identity, r=[("ps", bT), "pv"], w=[yk2],
                        bias=pvc(PV_GNB + c), scale=pvc(PV_GNW + c))
                    bB = bank()
                    MM(ps[bB][:, 0:128], oneblk[:], rkr[:, c, jj * 128:(jj + 1) * 128], r=["oneblk", (krkr, c)], w=[("ps", bB)])
                    TT(mix_t[:, 0:128], ps[bB][:, 0:128], vTb[:, c, jj * 128:(jj + 1) * 128], ALU.mult, r=[("ps", bB), (kvTb, c)], w=["mix_t"])
                    TT(yv, yv, mix_t[:, 0:128], ALU.add, r=[yk2, "mix_t"], w=[yk2])
                    TT(cat[:, c, t0 + jj * 128:t0 + (jj + 1) * 128], yv, Gg[:, c, jj * 128:(jj + 1) * 128], ALU.mult,
                       r=[yk2, (kGg, c)], w=[("cat", c)])

            steps = [chain_step(j) for j in range(NJ)] + [gn_step(jj) for jj in range(NJJ)]
            if defer:
                return steps
            for st_ in steps:
                for _ in st_:
                    pass
            return []

        def out_proj():
            sb_ = stats_begin()
            for m in range(KC):
                b = bank()
                for c in range(KC):
                    MM(ps[b][:], woutb[:, c, m * 128:(m + 1) * 128], cat[:, c, :], r=["woutb", ("cat", c)], w=[("ps", b)],
                       start=(c == 0), stop=(c == KC - 1))
                TT(xt[:, m, :], ps[b][:], xt[:, m, :], ALU.add, r=[("ps", b), ("xt", m)], w=[("xt", m)])
                stats_add(sb_, m)
            return sb_

        out_ops = []
        for it in range(ntiles):
            tsl = slice(it * T, (it + 1) * T)
            for hh in range(2):
                dma_in("pool", xt[:, hh * 4:(hh + 1) * 4, :], xT[:, hh * 4:(hh + 1) * 4, tsl], w=[("xt", kc) for kc in range(hh * 4, hh * 4 + 4)])
            sbk = None
            S.tag = f"t{it}.ffn1"
            if upto >= 1:
                rmsnorm_to(PV_N1, hn_out, "n1")
            if upto >= 2:
                sbk = ffn(0)
            S.tag = f"t{it}.mix"
            if it == 0:
                dump("x1", xt[:], r=[("xt", kc) for kc in range(KC)])
            if upto >= 3:
                rmsnorm_to(PV_NM, hn_out, "nm", b=sbk)
                sbk = None
                dfr = []
                for sub in range(T // TM):
                    S.tag = f"t{it}.mix{sub}"
                    dfr = mixer(sub, dfr, defer=(DEFER and sub == 0))
                if it == 0:
                    dump("cat", cat[:], r=[("cat", c) for c in range(KC)])
            S.tag = f"t{it}.outp"
            if upto >= 4:
                sbk = out_proj()
            if it == 0:
                dump("x2", xt[:], r=[("xt", kc) for kc in range(KC)])
            S.tag = f"t{it}.ffn2"
            if upto >= 5:
                rmsnorm_to(PV_N2, hn_out, "n2", b=sbk)
                sbk = ffn(1)
            rmsnorm_to(PV_NF, lambda kc: (ot_v[:, kc, :], ot_keys(kc)), "nf", b=sbk)
            for hh in range(2):
                rk = []
                for kc in range(hh * 4, hh * 4 + 4):
                    rk += ot_keys(kc)
                out_ops.append(S.add("pool", (lambda hh=hh, tsl=tsl: (lambda e: e.dma_start(out=outT[:, hh * 4:(hh + 1) * 4, tsl],
                                                                                         in_=ot_v[:, hh * 4:(hh + 1) * 4, :])))(),
                                     r=rk, w=[("outT", it, hh)], dma=True))
        fin_r = [("outT", it, hh) for it in range(ntiles) for hh in range(2)] + [("dbg", n) for n in dbg_d]
        S.add("pool", None, r=fin_r, w=["fin"])

        streams = S.emit(nc, sems, dsems, None)
        with nc.Block() as block:
            @block.tensor
            def _(e):
                _replay(e, "pe", streams["pe"], sems, dsems)

            @block.scalar
            def _(e):
                _replay(e, "act", streams["act"], sems, dsems)

            @block.vector
            def _(e):
                _replay(e, "dve", streams["dve"], sems, dsems)

            @block.gpsimd
            def _(e):
                _replay(e, "pool", streams["pool"], sems, dsems)

            @block.sync
            def _(e):
                _replay(e, "sp", streams["sp"], sems, dsems)
    nc._sched = S
    nc._sched_counts = dict(S.counts)
    nc._nops = len(S.ops)
    return nc


def _consts():
    co = np.zeros((128, CO_N), np.float32)
    p = np.arange(128)
    co[p, CO_ID + p] = 1.0
    co[:, CO_ONEB:CO_ONEB + 128] = (p[:, None] // 64 == p[None, :] // 64)
    r = (p % 64)[:, None]
    cidx = np.arange(64)[None, :]
    co[:, CO_MQM:CO_MQM + 64] = (cidx > r)
    co[:, CO_MQM + 64:CO_MQM + 128] = (cidx >= r)
    co[:, CO_MSL:CO_MSL + 64] = (r > cidx)
    co[:, CO_IB:CO_IB + 64] = (r == cidx)
    co[:, CO_MWS:CO_MWS + 128] = (p[:, None] <= p[None, :])
    return co


def _vec_cols(v):
    v = np.asarray(v, np.float32).reshape(-1, 128)
    return np.ascontiguousarray(v.T)


def _prep_shared(inp):
    f = lambda a: np.asarray(a, np.float32)
    pv = np.zeros((128, PV_N), np.float32)
    pv[:, PV_N1:PV_N1 + 8] = _vec_cols(f(inp["ffn1_norm"])[0])
    pv[:, PV_NM:PV_NM + 8] = _vec_cols(f(inp["mix_norm"])[0])
    pv[:, PV_N2:PV_N2 + 8] = _vec_cols(f(inp["ffn2_norm"])[0])
    pv[:, PV_NF:PV_NF + 8] = _vec_cols(f(inp["final_norm"]))
    mu = f(inp["mu_shift"])[0]
    pv[:, PV_MU:PV_MU + 14] = _vec_cols(mu)
    pv[:, PV_W0:PV_W0 + 4] = _vec_cols(f(inp["w0"])[0])
    pv[:, PV_A0:PV_A0 + 4] = _vec_cols(f(inp["a0"])[0])
    pv[:, PV_KK:PV_KK + 4] = _vec_cols(f(inp["k_k"])[0])
    pv[:, PV_KA:PV_KA + 4] = _vec_cols(f(inp["k_a"])[0])
    pv[:, PV_RK:PV_RK + 4] = _vec_cols(f(inp["r_k"])[0].reshape(-1))
    pv[:, PV_GNW:PV_GNW + 4] = _vec_cols(f(inp["gn_w"])[0])
    pv[:, PV_GNB:PV_GNB + 4] = _vec_cols(f(inp["gn_b"])[0])
    pv[:, PV_LNG:PV_LNG + 4] = _vec_cols(f(inp["sgu_ln_g"])[0])
    pv[:, PV_LNB:PV_LNB + 4] = _vec_cols(f(inp["sgu_ln_b"])[0])
    sh = {"pv": pv, "co": _consts()}
    for l, pre in enumerate(("ffn1", "ffn2")):
        w1 = f(inp[pre + "_w1"])[0].reshape(KC, 128, 11, 256)
        w3 = f(inp[pre + "_w3"])[0].reshape(KC, 128, 11, 256)
        w13 = np.stack([w1, w3], 0)
        sh[f"w13_{l}"] = np.ascontiguousarray(w13.transpose(3, 2, 0, 1, 4)).reshape(11, 128, 2 * KC * 256)
        w2 = f(inp[pre + "_w2"])[0].reshape(FC, 128, KC, 128)
        sh[f"w2_{l}"] = np.ascontiguousarray(w2.transpose(2, 1, 0, 3)).reshape(8, 128, FC * 128)
    win = f(inp["w_in"])[0].reshape(KC, 128, 22, 128)
    slabs = []
    for (ca, cb) in WIN_SLABS:
        sl = np.stack([win[:, :, ca, :], win[:, :, cb, :]], 2)
        slabs.append(sl.transpose(1, 0, 2, 3).reshape(128, KC * 256))
    sh["win"] = np.ascontiguousarray(np.stack(slabs, 0))
    wout = f(inp["w_out"])[0].reshape(KC, 128, 1024)
    sh["wout"] = np.ascontiguousarray(wout.transpose(1, 0, 2)).reshape(128, KC * 1024)
    sh["wup"] = np.ascontiguousarray(f(inp["w_lora_up"])[0])
    sh["aup"] = np.ascontiguousarray(f(inp["a_lora_up"])[0])
    sh["gup"] = np.ascontiguousarray(f(inp["g_lora_up"])[0])
    ws = f(inp["sgu_w"])[0]
    sh["wst"] = np.ascontiguousarray(ws.transpose(2, 0, 1)).reshape(128, 8 * 128)
    sbv = f(inp["sgu_b"])[0].reshape(4, 2, 128).transpose(1, 0, 2)
    sh["sbb"] = np.ascontiguousarray(np.repeat(sbv[:, None], 64, axis=1)).reshape(128, 4 * 128)
    return sh


def _x_layout(xb):
    return np.ascontiguousarray(xb.T.reshape(KC, 128, SEQ).transpose(1, 0, 2))


def _out_unlayout(o):
    return np.ascontiguousarray(o.transpose(1, 0, 2).reshape(D, SEQ).T)


_NC_CACHE = {}


def kernel(**inputs):
    x = np.asarray(inputs["x"], np.float32)
    sh = _prep_shared(inputs)
    if "nc" not in _NC_CACHE:
        _NC_CACHE["nc"] = build_nc()
    nc = _NC_CACHE["nc"]
    in_maps = []
    for b in range(NB):
        m = dict(sh)
        m["xT"] = _x_layout(x[b])
        in_maps.append(m)
    res = run_bass_kernel_spmd(nc, in_maps, core_ids=list(range(NB)))
    out = np.stack([_out_unlayout(np.asarray(res.results[b]["outT"])) for b in range(NB)], 0)
    return out.astype(np.float32)
```
